# Optimizing a Trainium2 kernel written in Bass

```python
import math
import jax, jax.numpy as jnp
from jax import lax
import numpy as np

D_MODEL = 2048
BATCH = 2
SEQ = 4096
DEPTH = 1

N_HEADS = 16
HEAD_DIM = D_MODEL // N_HEADS
N_KV_GROUPS = 4
GROUP_SIZE = N_HEADS // N_KV_GROUPS
Q_DIM = N_HEADS * HEAD_DIM
KV_DIM = N_KV_GROUPS * HEAD_DIM
CMP_LEN = 32
CMP_STRIDE = 16
SEL_BLOCK = 64
SEL_TOPN = 16
WINDOW = 512
Q_BLOCK = 64
SSM_CH = D_MODEL
SSM_GROUP = 16
SSM_NGROUPS = SSM_CH // SSM_GROUP
SSM_STATE = 64
DT_MIN = 1e-3
DT_MAX = 1e-1
D_FF = 4 * D_MODEL
EPS = 1e-6
NEG = -1e30
BIG = 1e6

PROJ_WIDTHS = [Q_DIM, KV_DIM, KV_DIM, KV_DIM, KV_DIM, KV_DIM, KV_DIM,
               3 * N_HEADS, SSM_CH, 2 * D_MODEL]
PROJ_COLS = sum(PROJ_WIDTHS)
PROJ_SPLITS = [int(v) for v in np.cumsum(PROJ_WIDTHS)[:-1]]

kernel_name = "hybrid_nsa_s5_gated_block"


def rmsnorm(x, g):
    x32 = x.astype(jnp.float32)
    y = x32 * lax.rsqrt(jnp.mean(x32 * x32, axis=-1, keepdims=True) + EPS)
    return (y * g.astype(jnp.float32)).astype(x.dtype)


def masked_softmax(s, mask):
    s = jnp.where(mask, s.astype(jnp.float32), NEG)
    m = jnp.max(s, axis=-1, keepdims=True)
    p = jnp.where(mask, jnp.exp(s - m), 0.0)
    return p / jnp.maximum(jnp.sum(p, axis=-1, keepdims=True), 1e-30)


def nsa_attention(q, k_cmp, v_cmp, k_sel, v_sel, k_win, v_win, gates,
                  w_ck1, w_ck2, pe_ck, w_cv1, w_cv2, pe_cv):
    bsz, seq = q.shape[0], q.shape[1]
    G, R, hd = N_KV_GROUPS, GROUP_SIZE, HEAD_DIM
    n_cmp = (seq - CMP_LEN) // CMP_STRIDE + 1
    n_sel = seq // SEL_BLOCK
    topn = min(SEL_TOPN, n_sel)
    scale = hd ** -0.5
    qg = q.reshape(bsz, seq, G, R, hd)
    gg = gates.reshape(bsz, seq, G, R, 3)
    kvs = [a.reshape(bsz, seq, G, hd) for a in (k_cmp, v_cmp, k_sel, v_sel, k_win, v_win)]
    k_cmp, v_cmp, k_sel, v_sel, k_win, v_win = kvs

    win_idx = jnp.arange(n_cmp)[:, None] * CMP_STRIDE + jnp.arange(CMP_LEN)[None, :]

    def compress(t, w1, w2, pe):
        blocks = t[:, win_idx] + pe[None, None, :, None, :]
        hid = jax.nn.gelu(jnp.einsum('bnlgd,lde->bnge', blocks, w1))
        return jnp.einsum('bnge,ef->bngf', hid, w2)

    kc = compress(k_cmp, w_ck1, w_ck2, pe_ck)
    vc = compress(v_cmp, w_cv1, w_cv2, pe_cv)
    cmp_start = jnp.arange(n_cmp) * CMP_STRIDE
    cmp_end = cmp_start + CMP_LEN - 1
    sel_start = jnp.arange(n_sel) * SEL_BLOCK
    overlap = ((cmp_start[:, None] < sel_start[None, :] + SEL_BLOCK)
               & (cmp_start[:, None] + CMP_LEN > sel_start[None, :])).astype(jnp.float32)

    ks_blocks = k_sel.reshape(bsz, n_sel, SEL_BLOCK, G, hd).transpose(0, 3, 1, 2, 4)
    vs_blocks = v_sel.reshape(bsz, n_sel, SEL_BLOCK, G, hd).transpose(0, 3, 1, 2, 4)
    pad = ((0, 0), (WINDOW, 0), (0, 0), (0, 0))
    kw_pad = jnp.pad(k_win, pad)
    vw_pad = jnp.pad(v_win, pad)
    gather = jax.vmap(jax.vmap(lambda blocks, ix: blocks[ix]))
    jj = jnp.arange(n_sel)

    def one_block(s0):
        t = s0 + jnp.arange(Q_BLOCK)
        qb = lax.dynamic_slice_in_dim(qg, s0, Q_BLOCK, axis=1) * scale
        gb = jax.nn.sigmoid(lax.dynamic_slice_in_dim(gg, s0, Q_BLOCK, axis=1))
        s_c = jnp.einsum('bqgrd,bngd->bgrqn', qb, kc)
        p_c = masked_softmax(s_c, cmp_end[None, :] <= t[:, None])
        o_c = jnp.einsum('bgrqn,bngd->bqgrd', p_c.astype(vc.dtype), vc)
        imp = jnp.einsum('bgrqn,nj->bgqj', p_c, overlap)
        cur = t // SEL_BLOCK
        valid = jj[None, :] <= cur[:, None]
        forced = (jj[None, :] == 0) | (jj[None, :] == cur[:, None]) | (jj[None, :] == cur[:, None] - 1)
        score = jnp.where(forced, BIG, jnp.where(valid, imp, -BIG))
        _, idx = lax.top_k(score, topn)
        sel_valid = jnp.take_along_axis(jnp.broadcast_to(valid, score.shape), idx, axis=-1)
        ksel = gather(ks_blocks, idx)
        vsel = gather(vs_blocks, idx)
        s_s = jnp.einsum('bqgrd,bgqnkd->bgrqnk', qb, ksel)
        kpos = idx[..., None] * SEL_BLOCK + jnp.arange(SEL_BLOCK)
        m_s = sel_valid[..., None] & (kpos <= t[None, None, :, None, None])
        p_s = masked_softmax(s_s.reshape(bsz, G, R, Q_BLOCK, topn * SEL_BLOCK),
                             m_s.reshape(bsz, G, 1, Q_BLOCK, topn * SEL_BLOCK))
        o_s = jnp.einsum('bgrqnk,bgqnkd->bqgrd',
                         p_s.reshape(bsz, G, R, Q_BLOCK, topn, SEL_BLOCK).astype(vsel.dtype), vsel)
        kw = lax.dynamic_slice_in_dim(kw_pad, s0, WINDOW + Q_BLOCK, axis=1)
        vw = lax.dynamic_slice_in_dim(vw_pad, s0, WINDOW + Q_BLOCK, axis=1)
        kp = s0 - WINDOW + jnp.arange(WINDOW + Q_BLOCK)
        diff = t[:, None] - kp[None, :]
        m_w = (diff >= 0) & (diff < WINDOW) & (kp[None, :] >= 0)
        s_w = jnp.einsum('bqgrd,bkgd->bgrqk', qb, kw)
        p_w = masked_softmax(s_w, m_w)
        o_w = jnp.einsum('bgrqk,bkgd->bqgrd', p_w.astype(vw.dtype), vw)
        return gb[..., 0:1] * o_c + gb[..., 1:2] * o_s + gb[..., 2:3] * o_w

    outs = lax.map(one_block, jnp.arange(seq // Q_BLOCK) * Q_BLOCK)
    return outs.transpose(1, 0, 2, 3, 4, 5).reshape(bsz, seq, Q_DIM)


def _complex_scan_op(left, right):
    a1r, a1i, b1r, b1i = left
    a2r, a2i, b2r, b2i = right
    return (a2r * a1r - a2i * a1i,
            a2r * a1i + a2i * a1r,
            a2r * b1r - a2i * b1i + b2r,
            a2r * b1i + a2i * b1r + b2i)


def s5_ssm(u, a_re, a_im, log_dt, b_re, b_im, c_re, c_im, d_skip):
    bsz, seq = u.shape[0], u.shape[1]
    ug = u.reshape(bsz, seq, SSM_NGROUPS, SSM_GROUP).astype(jnp.float32)
    are = a_re.astype(jnp.float32)
    aim = a_im.astype(jnp.float32)
    dt = jnp.exp(log_dt.astype(jnp.float32))[:, None]
    mag = jnp.exp(are * dt)
    lam_re = mag * jnp.cos(aim * dt)
    lam_im = mag * jnp.sin(aim * dt)
    den = are * are + aim * aim
    nr = lam_re - 1.0
    coef_re = (nr * are + lam_im * aim) / den
    coef_im = (lam_im * are - nr * aim) / den
    br = b_re.astype(jnp.float32)
    bi = b_im.astype(jnp.float32)
    bb_re = coef_re[..., None] * br - coef_im[..., None] * bi
    bb_im = coef_re[..., None] * bi + coef_im[..., None] * br
    bu_re = jnp.einsum('gpc,bsgc->bsgp', bb_re, ug)
    bu_im = jnp.einsum('gpc,bsgc->bsgp', bb_im, ug)
    a_r = jnp.broadcast_to(lam_re, bu_re.shape)
    a_i = jnp.broadcast_to(lam_im, bu_re.shape)
    _, _, x_re, x_im = lax.associative_scan(_complex_scan_op, (a_r, a_i, bu_re, bu_im), axis=1)
    y = (jnp.einsum('gcp,bsgp->bsgc', c_re.astype(jnp.float32), x_re)
         - jnp.einsum('gcp,bsgp->bsgc', c_im.astype(jnp.float32), x_im))
    y = y + d_skip.astype(jnp.float32).reshape(SSM_NGROUPS, SSM_GROUP) * ug
    return y.reshape(bsz, seq, SSM_CH).astype(u.dtype)


def setup_inputs(seed: int = 0) -> dict:
    key = jax.random.key(seed)
    ks = jax.random.split(key, 28)
    L = DEPTH
    nrm = lambda k, shape, s: jax.random.normal(k, shape, jnp.float32) * s
    x = nrm(ks[0], (BATCH, SEQ, D_MODEL), 1.0)
    c = nrm(ks[1], (BATCH, D_MODEL), 1.0)
    w_ada = nrm(ks[2], (L, D_MODEL, 6 * D_MODEL), 0.02)
    b_ada = nrm(ks[3], (L, 6 * D_MODEL), 0.02)
    g_mix = 1.0 + nrm(ks[4], (L, D_MODEL), 0.02)
    w_in = nrm(ks[5], (L, D_MODEL, PROJ_COLS), D_MODEL ** -0.5)
    w_ck1 = nrm(ks[6], (L, CMP_LEN, HEAD_DIM, HEAD_DIM), (CMP_LEN * HEAD_DIM) ** -0.5)
    w_ck2 = nrm(ks[7], (L, HEAD_DIM, HEAD_DIM), HEAD_DIM ** -0.5)
    pe_ck = nrm(ks[8], (L, CMP_LEN, HEAD_DIM), 0.1)
    w_cv1 = nrm(ks[9], (L, CMP_LEN, HEAD_DIM, HEAD_DIM), (CMP_LEN * HEAD_DIM) ** -0.5)
    w_cv2 = nrm(ks[10], (L, HEAD_DIM, HEAD_DIM), HEAD_DIM ** -0.5)
    pe_cv = nrm(ks[11], (L, CMP_LEN, HEAD_DIM), 0.1)
    a_re = -0.5 + nrm(ks[12], (L, SSM_NGROUPS, SSM_STATE), 0.01)
    a_im = jnp.broadcast_to(math.pi * jnp.arange(SSM_STATE, dtype=jnp.float32),
                            (L, SSM_NGROUPS, SSM_STATE)) + 0.0
    log_dt = jax.random.uniform(ks[13], (L, SSM_NGROUPS), jnp.float32,
                                math.log(DT_MIN), math.log(DT_MAX))
    b_re = nrm(ks[14], (L, SSM_NGROUPS, SSM_STATE, SSM_GROUP), (2 * SSM_GROUP) ** -0.5)
    b_im = nrm(ks[15], (L, SSM_NGROUPS, SSM_STATE, SSM_GROUP), (2 * SSM_GROUP) ** -0.5)
    c_re = nrm(ks[16], (L, SSM_NGROUPS, SSM_GROUP, SSM_STATE), (SSM_STATE) ** -0.5)
    c_im = nrm(ks[17], (L, SSM_NGROUPS, SSM_GROUP, SSM_STATE), (SSM_STATE) ** -0.5)
    d_skip = nrm(ks[18], (L, SSM_CH), 1.0)
    w_glu = nrm(ks[19], (L, SSM_CH, D_MODEL), SSM_CH ** -0.5)
    b_glu = nrm(ks[20], (L, D_MODEL), 0.02)
    w_out = nrm(ks[21], (L, D_MODEL, D_MODEL), D_MODEL ** -0.5)
    g_mlp = 1.0 + nrm(ks[22], (L, D_MODEL), 0.02)
    w_up = nrm(ks[23], (L, D_MODEL, D_FF), D_MODEL ** -0.5)
    w_down = nrm(ks[24], (L, D_FF, D_MODEL), D_FF ** -0.5)
    g_final = 1.0 + nrm(ks[25], (D_MODEL,), 0.02)
    return {"x": x, "c": c, "w_ada": w_ada, "b_ada": b_ada, "g_mix": g_mix, "w_in": w_in,
            "w_ck1": w_ck1, "w_ck2": w_ck2, "pe_ck": pe_ck,
            "w_cv1": w_cv1, "w_cv2": w_cv2, "pe_cv": pe_cv,
            "a_re": a_re, "a_im": a_im, "log_dt": log_dt, "b_re": b_re, "b_im": b_im,
            "c_re": c_re, "c_im": c_im, "d_skip": d_skip, "w_glu": w_glu, "b_glu": b_glu,
            "w_out": w_out, "g_mlp": g_mlp, "w_up": w_up, "w_down": w_down, "g_final": g_final}


def reference(x, c, w_ada, b_ada, g_mix, w_in, w_ck1, w_ck2, pe_ck, w_cv1, w_cv2, pe_cv,
              a_re, a_im, log_dt, b_re, b_im, c_re, c_im, d_skip, w_glu, b_glu,
              w_out, g_mlp, w_up, w_down, g_final):
    h = x
    cond = jax.nn.silu(c)
    for l in range(DEPTH):
        ada = (cond @ w_ada[l] + b_ada[l])[:, None, :]
        sh1, sc1, gt1, sh2, sc2, gt2 = jnp.split(ada, 6, axis=-1)
        u = rmsnorm(h, g_mix[l]) * (1.0 + sc1) + sh1
        proj = u @ w_in[l]
        q, kc, vc, ks_, vs_, kw, vw, g_nsa, u_ssm, g_merge = jnp.split(proj, PROJ_SPLITS, axis=-1)
        y_a = nsa_attention(q, kc, vc, ks_, vs_, kw, vw, g_nsa,
                            w_ck1[l], w_ck2[l], pe_ck[l], w_cv1[l], w_cv2[l], pe_cv[l])
        y_s = jax.nn.gelu(s5_ssm(u_ssm, a_re[l], a_im[l], log_dt[l], b_re[l], b_im[l],
                                 c_re[l], c_im[l], d_skip[l]))
        y_b = y_s * jax.nn.sigmoid(y_s @ w_glu[l] + b_glu[l])
        g_a, g_b = jnp.split(jax.nn.sigmoid(g_merge), 2, axis=-1)
        h = h + gt1 * ((g_a * y_a + g_b * y_b) @ w_out[l])
        u2 = rmsnorm(h, g_mlp[l]) * (1.0 + sc2) + sh2
        h = h + gt2 * (jnp.square(jax.nn.relu(u2 @ w_up[l])) @ w_down[l])
    return rmsnorm(h, g_final)
```

```python
import contextlib
import numpy as np
import concourse.bass as bass
import concourse.mybir as mybir
from concourse.bass_utils import run_bass_kernel_spmd

F32 = mybir.dt.float32
BF16 = mybir.dt.bfloat16
ALU = mybir.AluOpType
AF = mybir.ActivationFunctionType
AX = mybir.AxisListType

D = 2048
SEQ = 4096
NOWN = 1024
EPS = 1e-6
NEGM = -30000.0
PI = float(np.pi)
DEBUG = False

EPOCH = 12000
NSLOT = 10


class Sched:
    def __init__(self, nc):
        self.nc = nc
        self.ops = []
        self.phase = 0

    def add(self, eng, fn, r=(), w=(), dma=False):
        self.ops.append(dict(eng=eng, fn=fn, r=tuple(r) + (("PH", self.phase),), w=tuple(w), dma=dma))

    def barrier(self, fn):
        k = self.phase
        self.ops.append(dict(eng="pool", fn=fn, r=(), w=(("PH", k), ("PH", k + 1)), dma=False))
        self.phase = k + 1

    def emit(self, final_tokens=()):
        nc = self.nc
        engs = ["pe", "act", "dve", "pool", "sp"]
        ncomp = {e: 0 for e in engs}
        ndma = {e: 0 for e in engs}
        for op in self.ops:
            if op["dma"]:
                ndma[op["eng"]] += 1
            else:
                ncomp[op["eng"]] += 1
        stack = contextlib.ExitStack()
        csems = {e: [stack.enter_context(nc.semaphore(f"c_{e}_{i}")) for i in range(ncomp[e] // EPOCH + 1)] for e in engs}
        dsems = {e: [stack.enter_context(nc.semaphore(f"d_{e}_{i}")) for i in range(NSLOT)] for e in engs if ndma[e]}
        last_w, readers = {}, {}
        ccount = {e: 0 for e in engs}
        dcount = {e: 0 for e in engs}
        slot_uses = {e: [0] * NSLOT for e in engs}
        plan = {e: [] for e in engs}
        run = {e: {} for e in engs}
        for op in self.ops:
            e = op["eng"]
            deps = []
            for t in op["r"]:
                ev = last_w.get(t)
                if ev is not None:
                    deps.append(ev)
            for t in op["w"]:
                ev = last_w.get(t)
                if ev is not None:
                    deps.append(ev)
                deps.extend(readers.get(t, ()))
            if op["dma"]:
                k = dcount[e]
                dcount[e] += 1
                slot = k % NSLOT
                sem = dsems[e][slot]
                prev = slot_uses[e][slot]
                if prev > 0:
                    deps.append((sem, 16 * prev, "dma"))
                slot_uses[e][slot] = prev + 1
                ev = (sem, 16 * (prev + 1), "dma")
                inc = (sem, 16)
            else:
                k = ccount[e]
                ccount[e] += 1
                sem = csems[e][k // EPOCH]
                ev = (sem, (k % EPOCH) + 1, e)
                inc = (sem, 1)
            need = {}
            for (s_, v_, src) in deps:
                if e == "pe" and src == "pe" and not op["dma"]:
                    continue
                key = id(s_)
                if v_ > need.get(key, (None, 0))[1]:
                    need[key] = (s_, v_)
            waits = []
            rn = run[e]
            for key, (s_, v_) in need.items():
                if rn.get(key, 0) >= v_:
                    continue
                rn[key] = v_
                waits.append((s_, v_))
            plan[e].append((op["fn"], waits, inc))
            for t in op["r"]:
                readers.setdefault(t, []).append(ev)
            for t in op["w"]:
                last_w[t] = ev
                readers[t] = []
        finals = []
        for t in final_tokens:
            ev = last_w.get(t)
            if ev is not None:
                finals.append((ev[0], ev[1]))
        self.stats = {e: (ccount[e], dcount[e]) for e in engs}
        with stack:
            with nc.Block() as block:
                def run_engine(e, eng, tail=None):
                    for (fn, waits, inc) in plan[e]:
                        for (s_, v_) in waits:
                            eng.wait_ge(s_, v_)
                        fn(eng).then_inc(inc[0], inc[1])
                    if tail:
                        for (s_, v_) in tail:
                            eng.wait_ge(s_, v_)

                @block.tensor
                def _(eng):
                    run_engine("pe", eng)

                @block.scalar
                def _(eng):
                    run_engine("act", eng)

                @block.vector
                def _(eng):
                    run_engine("dve", eng)

                @block.gpsimd
                def _(eng):
                    run_engine("pool", eng)

                @block.sync
                def _(eng):
                    run_engine("sp", eng, tail=finals)


class Arena:
    def __init__(self, nc, cap):
        self.nc, self.cap, self.n = nc, int(nc.sbuf_top) - 64, 0
        self.off = (int(nc.sbuf_base) + 127) // 128 * 128

    def alloc(self, name, shape, dt):
        sz = int(np.prod(shape[1:])) * (2 if dt == BF16 else 4)
        sz = (sz + 63) // 64 * 64
        t = self.nc.alloc_sbuf_tensor_at(f"{name}{self.n}", list(shape), dt, offset=self.off)
        self.n += 1
        self.off += sz
        assert self.off <= self.cap, (name, self.off, self.cap)
        return t


def build(stop=None, debug=False):
    DEBUG = debug
    nc = bass.Bass("TRN2", target_bir_lowering=False)
    S = Sched(nc)

    need = {"s1": ["ccol", "w_ada", "bada", "gmix", "gmlp", "gfin", "bglu"]}
    need["s2"] = need["s1"] + ["xw", "w_in", "mtok"]
    need["s3"] = need["s2"] + ["w_in", "mtok"]
    need["s4"] = need["s3"] + ["are2", "aim2", "ldt2", "X1", "X2", "CRI", "CIR", "Dm"]
    declared = []
    S.declared = declared

    def din(name, shape, dt=F32):
        if stop in need and name not in need[stop]:
            return None
        declared.append(name)
        return nc.dram_tensor(name, list(shape), dt, kind="ExternalInput").ap()

    d_xw = din("xw", [SEQ, D])
    d_ccol = din("ccol", [128, 16])
    d_wada = din("w_ada", [D, 6 * D])
    d_bada = din("bada", [128, 96])
    d_gmix = din("gmix", [128, 16])
    d_gmlp = din("gmlp", [128, 16])
    d_gfin = din("gfin", [128, 16])
    d_bglu = din("bglu", [128, 16])
    d_win = din("w_in", [D, 11312])
    d_wck1 = din("w_ck1", [32, 128, 128])
    d_wck2 = din("w_ck2", [128, 128])
    d_pek = din("pek", [128, 32])
    d_wcv1 = din("w_cv1", [32, 128, 128])
    d_wcv2 = din("w_cv2", [128, 128])
    d_pev = din("pev", [128, 32])
    d_are = din("are2", [128, 64])
    d_aim = din("aim2", [128, 64])
    d_ldt = din("ldt2", [128, 64])
    d_X1 = din("X1", [128, 64, 16])
    d_X2 = din("X2", [128, 64, 16])
    d_CRI = din("CRI", [128, 64, 32])
    d_CIR = din("CIR", [128, 64, 32])
    d_Dm = din("Dm", [32, 64, 32])
    d_wglu = din("w_glu", [D, D])
    d_wout = din("w_out", [D, D])
    d_wup = din("w_up", [D, 4 * D])
    d_wdn = din("w_down", [4 * D, D])
    d_ov = din("ov", [128, 2, 64])
    d_E = din("Esel", [64, 32, 128])
    d_tri = din("tri4", [128, 128])
    d_tris = din("tris4", [128, 128])
    d_mc4 = din("mc4", [128, 8, 128])
    d_mtok = din("mtok", [128, SEQ])
    d_cmpb = din("cmpb", [128, 2])
    d_winb = din("winb", [128, 12])
    d_scmul = din("scmul", [128, 8, 64])
    d_scadd = din("scadd", [128, 8, 64])
    d_scval = din("scval", [128, 8, 64])
    d_out = nc.dram_tensor("out", [NOWN, D], F32, kind="ExternalOutput").ap()
    uscr = nc.dram_tensor("uscr", [16, 128, SEQ], BF16, kind="Internal").ap()
    usscr = nc.dram_tensor("usscr", [D, SEQ], BF16, kind="Internal").ap()
    h1scr = nc.dram_tensor("h1scr", [16, 128, NOWN], F32, kind="Internal").ap()
    bsscr = nc.dram_tensor("bsscr", [16, 128, NOWN], BF16, kind="Internal").ap()
    kscr = nc.dram_tensor("kscr", [4, 3, 128, SEQ], BF16, kind="Internal").ap()
    kwscr = nc.dram_tensor("kwscr", [4, 128, 1536], BF16, kind="Internal").ap()
    vsscr = nc.dram_tensor("vsscr", [4, 128, 32, 128], BF16, kind="Internal").ap()
    vwscr = nc.dram_tensor("vwscr", [4, 128, 12, 128], BF16, kind="Internal").ap()
    qscr = nc.dram_tensor("qscr", [4, 128, 8, 4, 128], BF16, kind="Internal").ap()
    gscr = nc.dram_tensor("gscr", [4, 128, 8, 12], F32, kind="Internal").ap()
    dbg = {}
    if DEBUG:
        dbg["ada"] = nc.dram_tensor("dbg_ada", [128, 96], F32, kind="ExternalOutput").ap()
        dbg["uT"] = nc.dram_tensor("dbg_uT", [128, 16, 512], BF16, kind="ExternalOutput").ap()
        dbg["us"] = nc.dram_tensor("dbg_us", [128, 16, 512], BF16, kind="ExternalOutput").ap()
        dbg["yssm"] = nc.dram_tensor("dbg_yssm", [128, 8, D], BF16, kind="ExternalOutput").ap()
        dbg["yaT"] = nc.dram_tensor("dbg_yaT", [128, 16, NOWN], BF16, kind="ExternalOutput").ap()
    dbg_written = []

    def dump(name, ap, shape, dt, toks):
        if not DEBUG:
            return
        t_ = nc.dram_tensor("dbg_" + name, list(shape), dt, kind="ExternalOutput").ap()
        dma("sp", t_, ap, toks, ["dbg_" + name])
        dbg_written.append("dbg_" + name)

    def finish():
        S.emit(final_tokens=[f"out{q_}" for q_ in range(8)] + dbg_written)
        return nc, S

    d_win3 = d_win.rearrange("(k p) n -> p k n", p=128) if d_win is not None else None

    A = Arena(nc, int(nc.sbuf_bytes_remaining) - 256)
    ps = [nc.alloc_psum_tensor(f"ps{i}", [128, 512], F32) for i in range(7)]
    psb = nc.alloc_psum_tensor("psb", [128, 1024], BF16)
    psb2 = ps[6][:, :].bitcast(BF16)

    def mm(out, lhsT, rhs, start, stop, r, w):
        S.add("pe", lambda e: e.matmul(out, lhsT=lhsT, rhs=rhs, start=start, stop=stop), r, w)

    def tr(out, in_, ident, r, w):
        S.add("pe", lambda e: e.transpose(out=out, in_=in_, identity=ident), r, w)

    def actv(out, in_, func, r, w, bias=None, scale=None, accum=None):
        kw = {}
        if bias is not None:
            kw["bias"] = bias
        if scale is not None:
            kw["scale"] = scale
        if accum is not None:
            kw["accum_out"] = accum
        S.add("act", lambda e: e.activation(out=out, in_=in_, func=func, **kw), r, w)

    def tt(eng, out, in0, in1, op, r, w):
        S.add(eng, lambda e: e.tensor_tensor(out=out, in0=in0, in1=in1, op=op), r, w)

    def ts(eng, out, in0, s1, s2, op0, op1, r, w):
        if op1 is None:
            S.add(eng, lambda e: e.tensor_scalar(out=out, in0=in0, scalar1=s1, scalar2=None, op0=op0), r, w)
        else:
            S.add(eng, lambda e: e.tensor_scalar(out=out, in0=in0, scalar1=s1, scalar2=s2, op0=op0, op1=op1), r, w)

    def stt(eng, out, in0, sc, in1, op0, op1, r, w):
        S.add(eng, lambda e: e.scalar_tensor_tensor(out=out, in0=in0, scalar=sc, in1=in1, op0=op0, op1=op1), r, w)

    def cp(eng, out, in_, r, w):
        S.add(eng, lambda e: e.tensor_copy(out=out, in_=in_), r, w)

    def ms(eng, ap, val, w):
        S.add(eng, lambda e: e.memset(ap, val), (), w)

    def dma(q, out, in_, r, w):
        S.add(q, lambda e: e.dma_start(out=out, in_=in_), r, w, dma=True)

    def load(name, shape, src, dt=F32, q="sp"):
        t = A.alloc(name, shape, dt)
        dma(q, t[:], src, (), [name])
        return t

    identf = A.alloc("identf", [128, 128], F32)
    ident = A.alloc("ident", [128, 128], BF16)
    onesf = A.alloc("onesf", [128, 128], F32)
    one11 = A.alloc("one11", [1, 1], F32)
    sgn = A.alloc("sgn", [128, 1], F32)
    nsgn = A.alloc("nsgn", [128, 1], F32)
    bartile = A.alloc("bartile", [128, 1], F32)
    ms("pool", identf[:], 1.0, ["identf"])
    S.add("pool", lambda e: e.affine_select(out=identf[:], in_=identf[:], pattern=[[-1, 128]], compare_op=ALU.is_equal,
                                            fill=0.0, base=0, channel_multiplier=1), ["identf"], ["identf"])
    cp("dve", ident[:], identf[:], ["identf"], ["ident"])
    ms("dve", onesf[:], 1.0, ["onesf"])
    ms("dve", one11[:], 1.0, ["one11"])
    ms("dve", sgn[0:64, :], 1.0, ["sgn"])
    ms("dve", sgn[64:128, :], -1.0, ["sgn"])
    ms("dve", nsgn[0:64, :], -1.0, ["nsgn"])
    ms("dve", nsgn[64:128, :], 1.0, ["nsgn"])
    gmix = load("gmix", [128, 16], d_gmix)
    gmlp = load("gmlp", [128, 16], d_gmlp)
    gfin = load("gfin", [128, 16], d_gfin)
    bglu = load("bglu", [128, 16], d_bglu)
    bada = load("bada", [128, 96], d_bada)
    ccol = load("ccol", [128, 16], d_ccol)
    ada = A.alloc("ada", [128, 96], F32)
    m1s = A.alloc("m1s", [128, 16], F32)
    m2s = A.alloc("m2s", [128, 16], F32)
    persist = A.off

    def barrier():
        S.barrier(lambda e: e.memset(bartile[:], 0.0))

    cond = A.alloc("cond", [128, 16], F32)
    actv(cond[:], ccol[:], AF.Silu, ["ccol"], ["cond"])
    wab = [A.alloc("wab", [128, 16, 512], F32) for _ in range(2)]
    arow = [A.alloc("arow", [1, 512], F32) for _ in range(2)]
    d_wada3 = d_wada.rearrange("(k p) n -> p k n", p=128)
    for nt in range(24):
        wt = wab[nt % 2]
        tk = f"wab{nt % 2}"
        dma("sp", wt[:], d_wada3[:, :, nt * 512:(nt + 1) * 512], (), [tk])
        for k in range(16):
            mm(ps[0][0:1, :], cond[:, k:k + 1], wt[:, k, :], k == 0, k == 15, ["cond", tk], ["ps0"])
        ar = arow[nt % 2]
        ak = f"arow{nt % 2}"
        actv(ar[:], ps[0][0:1, :], AF.Copy, ["ps0"], [ak])
        for c4 in range(4):
            col = nt * 4 + c4
            mm(ps[6][:, col:col + 1], ar[0:1, c4 * 128:(c4 + 1) * 128], one11[0:1, 0:1], True, True, [ak, "one11"], ["ps6"])
    tt("dve", ada[:], ps[6][:, 0:96], bada[:], ALU.add, ["ps6", "bada"], ["ada"])
    tmp16 = A.alloc("tmp16", [128, 16], F32)
    ts("dve", tmp16[:], ada[:, 16:32], 1.0, None, ALU.add, None, ["ada"], ["tmp16"])
    tt("dve", m1s[:], tmp16[:], gmix[:], ALU.mult, ["tmp16", "gmix"], ["m1s"])
    ts("dve", tmp16[:], ada[:, 64:80], 1.0, None, ALU.add, None, ["ada"], ["tmp16"])
    tt("dve", m2s[:], tmp16[:], gmlp[:], ALU.mult, ["tmp16", "gmlp"], ["m2s"])
    if DEBUG:
        dma("sp", dbg["ada"], ada[:], ["ada"], ["dbg_ada"])
        dbg_written.append("dbg_ada")
    if stop == "s1":
        return finish()
    barrier()
    A.off = persist

    xtb = [A.alloc("xt", [128, 4, D], F32) for _ in range(2)]
    xn = A.alloc("xn", [128, 4, D], BF16)
    junk = A.alloc("junk", [128, D], BF16)
    uTb = [A.alloc("uT", [128, 16, 512], BF16) for _ in range(2)]
    ssq = A.alloc("ssq", [128, 32], F32)
    rstd = A.alloc("rstd", [128, 32], F32)
    ms("dve", ssq[:], 0.0, ["ssq"])
    uscr3 = uscr.rearrange("c p t -> p c t")
    wssm = A.alloc("wssm", [128, 16, D], BF16)
    for i in range(4):
        dma("pool", wssm[:, :, i * 512:(i + 1) * 512], d_win3[:, :, 5168 + i * 512:5168 + (i + 1) * 512], (), ["wssm"])
    mtok = A.alloc("mtok", [128, SEQ], BF16)
    dma("pool", mtok[:], d_mtok, (), ["mtok"])
    usT = A.alloc("usT", [128, 16, 512], BF16)
    usscr3 = usscr.rearrange("(c p) t -> p c t", p=128)
    import os
    for T in range(int(os.environ.get("KLIM", "8"))):
        xt = xtb[T % 2]
        xk = f"xt{T % 2}"
        dma("sp", xt[:], d_xw[T * 512:(T + 1) * 512, :].rearrange("(a p) f -> p a f", p=128), (), [xk])
        KS = os.environ.get("KSKIP", "")
        if "norm" in KS:
            ms("dve", xn[:], 1.0, ["xn"])
        for a in range(4 if "norm" not in KS else 0):
            actv(junk[:], xt[:, a, :], AF.Square, [xk, "ssq"], ["junk", "ssq"], accum=ssq[:, T * 4 + a:T * 4 + a + 1])
        if "norm" not in KS:
            ts("dve", rstd[:, T * 4:T * 4 + 4], ssq[:, T * 4:T * 4 + 4], 1.0 / D, EPS, ALU.mult, ALU.add, ["ssq"], ["rstd"])
            actv(rstd[:, T * 4:T * 4 + 4], rstd[:, T * 4:T * 4 + 4], AF.Sqrt, ["rstd"], ["rstd"])
            S.add("dve", (lambda o, i: (lambda e: e.reciprocal(out=o, in_=i)))(rstd[:, T * 4:T * 4 + 4], rstd[:, T * 4:T * 4 + 4]), ["rstd"], ["rstd"])
        for a in range(4 if "norm" not in KS else 0):
            actv(xn[:, a, :], xt[:, a, :], AF.Copy, [xk, "rstd"], ["xn"], scale=rstd[:, T * 4 + a:T * 4 + a + 1])
        uT = uTb[T % 2]
        uk = f"uT{T % 2}"
        if "tr" in KS:
            ms("dve", uT[:], 2.0, [uk])
        for f in range(16 if "tr" not in KS else 0):
            hk = "psb" if f % 2 == 0 else "ps6"
            pv = psb[:, 0:512] if f % 2 == 0 else psb2[:, 0:512]
            for a in range(4):
                tr(pv[:, a * 128:(a + 1) * 128], xn[:, a, f * 128:(f + 1) * 128], ident[:], ["xn", "ident"], [hk])
            actv(uT[:, f, :], pv, AF.Identity, [hk, "m1s", "ada"], [uk], bias=ada[:, f:f + 1], scale=m1s[:, f:f + 1])
        dma("sp", uscr3[:, :, T * 512:(T + 1) * 512], uT[:], [uk], ["uscr"])
        if DEBUG and T == int(os.environ.get("KLIM", "8")) - 1:
            dma("sp", dbg["uT"], uT[:], [uk], ["dbg_uT"])
            dbg_written.append("dbg_uT")
        for cc in range(16):
            pk = f"ps{cc % 2}"
            for k in range(16):
                mm(ps[cc % 2][:, :], wssm[:, k, cc * 128:(cc + 1) * 128], uT[:, k, :], k == 0, k == 15, ["wssm", uk], [pk])
            tt("dve", usT[:, cc, :], ps[cc % 2][:, :], mtok[:, T * 512:(T + 1) * 512], ALU.mult, [pk, "mtok"], ["usT"])
        dma("sp", usscr3[:, :, T * 512:(T + 1) * 512], usT[:], ["usT"], ["usscr"])
        if DEBUG and T == 7:
            dma("sp", dbg["us"], usT[:], ["usT"], ["dbg_us"])
            dbg_written.append("dbg_us")
    if stop in ("s2", "s3"):
        return finish()
    barrier()
    A.off = persist

    Y = A.alloc("Y", [128, 8, D], BF16)
    ssm_base = A.off
    TBb = [A.alloc("TB", [128, 2, NOWN], F32)] * 2
    wre = A.alloc("wre", [128, NOWN], F32)
    wim = A.alloc("wim", [128, NOWN], F32)
    zbuf = A.alloc("zbuf", [128, NOWN], F32)
    E_all = A.alloc("E_all", [128, 16, 2, 128], F32)
    Bs_all = A.alloc("Bs_all", [128, 16, 2, 128], F32)
    Bs_bf = A.alloc("Bs_bf", [128, 16, 2, 128], BF16)
    A_all = A.alloc("A_all", [128, 16, 2, 24], F32)
    csE = A.alloc("csE", [128, 16, 2], F32)
    csE2 = A.alloc("csE2", [128, 16, 2], F32)
    csE4 = A.alloc("csE4", [128, 16, 2], F32)
    cmb = [A.alloc("cm", [128, 16, 2], F32) for _ in range(2)]
    bt1 = wre[:].rearrange("p (g n) -> p g n", n=64)
    bt2 = wim[:].rearrange("p (g n) -> p g n", n=64)
    sq1 = A.alloc("sq1", [128, 16], F32)
    sq2 = A.alloc("sq2", [128, 16], F32)
    utokb = [A.alloc("utok", [128, 24, 32], BF16) for _ in range(2)]
    BtTb = [A.alloc("BtT", [128, 2, 128], BF16) for _ in range(2)]
    _bsflat = Bs_all[:].rearrange("p g a n -> p (g a n)")
    P1, P2, P3, P4 = [_bsflat[:, k_ * 768:(k_ + 1) * 768].rearrange("p (c i) -> p c i", i=32) for k_ in range(4)]
    Xc = A.alloc("Xc", [128, 2, 32], F32)
    q1t = A.alloc("q1t", [128, 32], F32)
    q2t = A.alloc("q2t", [128, 32], F32)
    xs = A.alloc("xs", [128, 4], F32)
    zi = A.alloc("zi", [128, 2], F32)
    t1 = A.alloc("t1", [128, 512], F32)
    t2 = A.alloc("t2", [128, 512], F32)
    ptA, ptB = t1, t2
    wAp = A.alloc("wAp", [128, 16, 908], BF16)
    uTp = A.alloc("uTp", [128, 16, 512], BF16)
    stg = [A.alloc("stg", [128, 512], BF16) for _ in range(4)]
    stgg = A.alloc("stgg", [128, 4, 12], F32)
    csb = [A.alloc("cs", [128, 2], F32) for _ in range(2)]
    cst = A.alloc("cst", [128, 2], F32)
    Zcr = A.alloc("Zcr", [128, NOWN], BF16)
    Zsr = A.alloc("Zsr", [128, NOWN], BF16)
    Zci = A.alloc("Zci", [128, NOWN], BF16)
    Zsi = A.alloc("Zsi", [128, NOWN], BF16)
    upb = [A.alloc("up", [32, SEQ], BF16) for _ in range(2)]
    BTre = A.alloc("BTre", [32, 16, 128], BF16)
    BTim = A.alloc("BTim", [32, 16, 128], BF16)
    Ha = A.alloc("Ha", [128, 16, 32], BF16)
    Hni = A.alloc("Hni", [128, 16, 32], BF16)
    Hnr = A.alloc("Hnr", [128, 16, 32], BF16)
    Dmb = A.alloc("Dmb", [32, 16, 32], BF16)
    Mre = A.alloc("Mre", [128, 16, 32], BF16)
    Mim = A.alloc("Mim", [128, 16, 32], BF16)
    blk_base = A.off

    def gen_tables(pair, pl, sm, tk):
        TB = TBb[0]
        TBk = "TB0"
        cp("dve", TB[:, :, 0:128], E_all[:, pl, :, :], ["E_all"], [TBk])
        for (m, ct, ck) in [(128, csE, "csE"), (256, csE2, "csE2"), (512, csE4, "csE4")]:
            c_ap, s_ap = ct[:, pl, 0:1], ct[:, pl, 1:2]
            Co, So = TB[:, 0, 0:m], TB[:, 1, 0:m]
            ts("dve", ptA[:, 0:m], So, s_ap, None, ALU.mult, None, [TBk, ck], ["t1"])
            stt("dve", TB[:, 0, m:2 * m], Co, c_ap, ptA[:, 0:m], ALU.mult, ALU.subtract, [TBk, ck, "t1"], [TBk])
            ts("dve", ptB[:, 0:m], So, c_ap, None, ALU.mult, None, [TBk, ck], ["t2"])
            stt("dve", TB[:, 1, m:2 * m], Co, s_ap, ptB[:, 0:m], ALU.mult, ALU.add, [TBk, ck, "t2"], [TBk])

    def sq_mult(dst, dk, src, sk):
        tt("dve", sq1[:], src[:, :, 1], src[:, :, 1], ALU.mult, [sk], ["sq1"])
        tt("dve", sq2[:], src[:, :, 0], src[:, :, 0], ALU.mult, [sk], ["sq2"])
        tt("dve", dst[:, :, 0], sq2[:], sq1[:], ALU.subtract, ["sq1", "sq2"], [dk])
        tt("dve", sq1[:], src[:, :, 0], src[:, :, 1], ALU.mult, [sk], ["sq1"])
        tt("dve", dst[:, :, 1], sq1[:], sq1[:], ALU.add, ["sq1"], [dk])

    def block_table(tab, tname, L, m_re0, m_im0, mkeys, left, extra_w=[]):
        idx0 = L - 1 if left else 0
        ms("dve", tab[:, :, 0, idx0:idx0 + 1], 1.0, [tname] + extra_w)
        ms("dve", tab[:, :, 1, idx0:idx0 + 1], 0.0, [tname] + extra_w)
        cp("dve", cmb[0][:, :, 0], m_re0, mkeys, ["cm0"])
        cp("dve", cmb[0][:, :, 1], m_im0, mkeys, ["cm0"])
        m, k = 1, 0
        while m < L:
            cur = cmb[k % 2]
            ck = f"cm{k % 2}"
            if left:
                d0, d1 = max(0, L - 2 * m), L - m
                s0 = d0 + m
            else:
                d0, d1 = m, min(L, 2 * m)
                s0 = 0
            n = d1 - d0
            mre = cur[:, :, 0:1].to_broadcast([128, 16, n])
            mim = cur[:, :, 1:2].to_broadcast([128, 16, n])
            sre, sim = tab[:, :, 0, s0:s0 + n], tab[:, :, 1, s0:s0 + n]
            tt("dve", bt1[:, :, 0:n], sre, mre, ALU.mult, [tname, ck], ["wre"])
            tt("dve", bt2[:, :, 0:n], sim, mim, ALU.mult, [tname, ck], ["wim"])
            tt("dve", tab[:, :, 0, d0:d1], bt1[:, :, 0:n], bt2[:, :, 0:n], ALU.subtract, ["wre", "wim"], [tname] + extra_w)
            tt("dve", bt1[:, :, 0:n], sre, mim, ALU.mult, [tname, ck], ["wre"])
            tt("dve", bt2[:, :, 0:n], sim, mre, ALU.mult, [tname, ck], ["wim"])
            tt("dve", tab[:, :, 1, d0:d1], bt1[:, :, 0:n], bt2[:, :, 0:n], ALU.add, ["wre", "wim"], [tname] + extra_w)
            m *= 2
            nxt = cmb[(k + 1) % 2]
            nk = f"cm{(k + 1) % 2}"
            tt("dve", sq1[:], cur[:, :, 1], cur[:, :, 1], ALU.mult, [ck], ["sq1"])
            tt("dve", sq2[:], cur[:, :, 0], cur[:, :, 0], ALU.mult, [ck], ["sq2"])
            tt("dve", nxt[:, :, 0], sq2[:], sq1[:], ALU.subtract, ["sq1", "sq2"], [nk])
            tt("dve", sq1[:], cur[:, :, 0], cur[:, :, 1], ALU.mult, [ck], ["sq1"])
            tt("dve", nxt[:, :, 1], sq1[:], sq1[:], ALU.add, ["sq1"], [nk])
            k += 1
        return cmb[k % 2], f"cm{k % 2}"

    def interleave(a0, a1, a2):
        la, lb = S.ops[a0:a1], S.ops[a1:a2]
        out, i, j = [], 0, 0
        while i < len(la) or j < len(lb):
            if j >= len(lb) or (i < len(la) and i * len(lb) <= j * len(la)):
                out.append(la[i]); i += 1
            else:
                out.append(lb[j]); j += 1
        S.ops[a0:a2] = out

    SCALE = 128.0 ** -0.5
    kvbase = [2048, 2560, 3072, 3584, 4096, 4608]

    def gen_proj(g):
        cnt = [0, 0]

        def nxt_ps():
            i_ = 5 + cnt[0] % 2
            cnt[0] += 1
            return ps[i_], f"ps{i_}"

        def nxt_stg():
            i_ = cnt[1] % 4
            cnt[1] += 1
            return stg[i_], f"stg{i_}"

        marks = []

        def mark():
            marks.append(len(S.ops))

        def fmaj(col0, dst, T, scale=None, view=None):
            mark()
            p_, pk = nxt_ps()
            for k in range(16):
                mm(p_[:, :], wAp[:, k, col0:col0 + 128], uTp[:, k, :], k == 0, k == 15, ["wAp", "uTp"], [pk])
            st_, sk = nxt_stg()
            actv(st_[:], p_[:, :], AF.Copy, [pk], [sk], scale=scale)
            dma("sp", dst, st_[:] if view is None else st_[:].rearrange(view, t=128), [sk], [f"kscr{g}"])

        def tmaj(col0, dsts, T):
            st_, sk = nxt_stg()
            for a in range(4):
                mark()
                p_, pk = nxt_ps()
                for k in range(16):
                    mm(p_[:, 0:128], uTp[:, k, a * 128:(a + 1) * 128], wAp[:, k, col0:col0 + 128], k == 0, k == 15, ["wAp", "uTp"], [pk])
                actv(st_[:, a * 128:(a + 1) * 128], p_[:, 0:128], AF.Copy, [pk], [sk])
            dma("sp", dsts, st_[:].rearrange("p (a t) -> p a t", t=128), [sk], [f"kscr{g}"])

        start = len(S.ops)
        dma("pool", wAp[:, :, 0:512], d_win3[:, :, 512 * g:512 * g + 512], (), ["wAp"])
        dma("pool", wAp[:, :, 512:524], d_win3[:, :, 5120 + 12 * g:5120 + 12 * g + 12], (), ["wAp"])
        for i in range(3):
            dma("pool", wAp[:, :, 524 + 128 * i:524 + 128 * i + 128], d_win3[:, :, kvbase[i] + 128 * g:kvbase[i] + 128 * g + 128], (), ["wAp"])
        for T in range(8):
            dma("sp", uTp[:], uscr3[:, :, T * 512:(T + 1) * 512], ["uscr"], ["uTp"])
            for i in range(3):
                fmaj(524 + 128 * i, kscr[g, i, :, T * 512:(T + 1) * 512], T)
            if T >= 6:
                qi0 = (T - 6) * 4
                for h in range(4):
                    fmaj(h * 128, qscr[g, :, qi0:qi0 + 4, h, :], T, scale=SCALE, view="p (a t) -> p a t")
                mark()
                for a in range(4):
                    p_, pk = nxt_ps()
                    for k in range(16):
                        mm(p_[:, 0:12], uTp[:, k, a * 128:(a + 1) * 128], wAp[:, k, 512:524], k == 0, k == 15, ["wAp", "uTp"], [pk])
                    actv(stgg[:, a, :], p_[:, 0:12], AF.Sigmoid, [pk], ["stgg"])
                dma("sp", gscr[g, :, qi0:qi0 + 4, :], stgg[:], ["stgg"], [f"kscr{g}"])
        for i in range(3):
            dma("pool", wAp[:, :, 128 * i:128 * i + 128], d_win3[:, :, kvbase[3 + i] + 128 * g:kvbase[3 + i] + 128 * g + 128], (), ["wAp"])
        for T in range(8):
            dma("sp", uTp[:], uscr3[:, :, T * 512:(T + 1) * 512], ["uscr"], ["uTp"])
            tmaj(0, vsscr[g, :, T * 4:T * 4 + 4, :], T)
            if T >= 5:
                fmaj(128, kwscr[g, :, (T - 5) * 512:(T - 5) * 512 + 512], T)
                tmaj(256, vwscr[g, :, (T - 5) * 4:(T - 5) * 4 + 4, :], T)
        ops = S.ops[start:]
        del S.ops[start:]
        bounds = [0] + [m_ - start for m_ in marks[1:]] + [len(ops)]
        return [ops[bounds[i_]:bounds[i_ + 1]] for i_ in range(len(bounds) - 1)]

    pq = []

    def proj_slot(n=1):
        for _ in range(n):
            if pq:
                S.ops.extend(pq.pop(0))

    for B in range(4):
        A.off = blk_base
        pq.extend(gen_proj(B))
        per_pair = (len(pq) + 15) // 16
        gs = slice(16 * B, 16 * B + 16)
        sfx = f"_{B}"
        are = load("are" + sfx, [128, 16], d_are[:, gs])
        aim = load("aim" + sfx, [128, 16], d_aim[:, gs])
        ldt = load("ldt" + sfx, [128, 16], d_ldt[:, gs])
        XR = load("XR" + sfx, [128, 16, 16], d_X1[:, gs, :])
        XI = load("XI" + sfx, [128, 16, 16], d_X2[:, gs, :])
        HCR = load("HCR" + sfx, [128, 16, 32], d_CRI[:, gs, :])
        HCI = load("HCI" + sfx, [128, 16, 32], d_CIR[:, gs, :])
        Dmf = load("Dmf" + sfx, [32, 16, 32], d_Dm[:, gs, :])
        sm = {}
        for nm in ["dt", "adt", "mag", "ang", "q1", "q2", "angs", "angc", "s1", "c1", "lre", "lim", "nr", "den", "cre", "cim", "u1", "u2"]:
            sm[nm] = A.alloc(nm + sfx, [128, 16], F32)
        big1 = A.alloc("big1" + sfx, [128, 16, 16], F32)
        big2 = A.alloc("big2" + sfx, [128, 16, 16], F32)
        SB = A.alloc("SB" + sfx, [128, 16, 16], F32)
        tk = lambda n: n + sfx

        def el(eng, o, a, b, op):
            tt(eng, sm[o][:], sm[a][:], sm[b][:], op, [tk(a), tk(b)], [tk(o)])

        actv(sm["dt"][:], ldt[:], AF.Exp, ["ldt" + sfx], [tk("dt")])
        tt("dve", sm["adt"][:], are[:], sm["dt"][:], ALU.mult, ["are" + sfx, tk("dt")], [tk("adt")])
        actv(sm["mag"][:], sm["adt"][:], AF.Exp, [tk("adt")], [tk("mag")])
        tt("dve", sm["ang"][:], aim[:], sm["dt"][:], ALU.mult, ["aim" + sfx, tk("dt")], [tk("ang")])

        def reduce_angle(src, dst):
            ts("dve", sm["q1"][:], sm[src][:], PI, None, ALU.is_gt, None, [tk(src)], [tk("q1")])
            for th in (3 * PI, 5 * PI, 7 * PI):
                ts("dve", sm["q2"][:], sm[src][:], th, None, ALU.is_gt, None, [tk(src)], [tk("q2")])
                el("dve", "q1", "q1", "q2", ALU.add)
            stt("dve", sm[dst][:], sm["q1"][:], -2 * PI, sm[src][:], ALU.mult, ALU.add, [tk("q1"), tk(src)], [tk(dst)])

        reduce_angle("ang", "angs")
        actv(sm["s1"][:], sm["angs"][:], AF.Sin, [tk("angs")], [tk("s1")])
        ts("dve", sm["angc"][:], sm["ang"][:], PI / 2, None, ALU.add, None, [tk("ang")], [tk("angc")])
        reduce_angle("angc", "angs")
        actv(sm["c1"][:], sm["angs"][:], AF.Sin, [tk("angs")], [tk("c1")])
        el("dve", "lre", "mag", "c1", ALU.mult)
        el("dve", "lim", "mag", "s1", ALU.mult)
        ts("dve", sm["nr"][:], sm["lre"][:], -1.0, None, ALU.add, None, [tk("lre")], [tk("nr")])
        tt("dve", sm["den"][:], are[:], are[:], ALU.mult, ["are" + sfx], [tk("den")])
        tt("dve", sm["u1"][:], aim[:], aim[:], ALU.mult, ["aim" + sfx], [tk("u1")])
        el("dve", "den", "den", "u1", ALU.add)
        S.add("dve", (lambda o: (lambda e: e.reciprocal(out=o, in_=o)))(sm["den"][:]), [tk("den")], [tk("den")])
        tt("dve", sm["u1"][:], sm["nr"][:], are[:], ALU.mult, [tk("nr"), "are" + sfx], [tk("u1")])
        tt("dve", sm["u2"][:], sm["lim"][:], aim[:], ALU.mult, [tk("lim"), "aim" + sfx], [tk("u2")])
        el("dve", "u1", "u1", "u2", ALU.add)
        el("dve", "cre", "u1", "den", ALU.mult)
        tt("dve", sm["u1"][:], sm["lim"][:], are[:], ALU.mult, [tk("lim"), "are" + sfx], [tk("u1")])
        tt("dve", sm["u2"][:], sm["nr"][:], aim[:], ALU.mult, [tk("nr"), "aim" + sfx], [tk("u2")])
        el("dve", "u1", "u1", "u2", ALU.subtract)
        el("dve", "cim", "u1", "den", ALU.mult)

        def bc(n):
            return sm[n][:].unsqueeze(2).to_broadcast([128, 16, 16])

        def build_M(ka, xa, kb, xb, op, Mdst, mname):
            tt("dve", big1[:], xa[:], bc(ka), ALU.mult, [tk(ka), "XR" + sfx, "XI" + sfx], ["big1" + sfx])
            tt("dve", big2[:], xb[:], bc(kb), ALU.mult, [tk(kb), "XR" + sfx, "XI" + sfx], ["big2" + sfx])
            tt("dve", SB[:], big1[:], big2[:], op, ["big1" + sfx, "big2" + sfx], ["SB" + sfx])
            ms("dve", Mdst[:], 0.0, [mname])
            cp("dve", Mdst[0:64, :, 0:16], SB[0:64, :, :], ["SB" + sfx], [mname])
            cp("dve", Mdst[64:128, :, 16:32], SB[64:128, :, :], ["SB" + sfx], [mname])

        build_M("cre", XR, "cim", XI, ALU.subtract, Mre, "Mre")
        for g0 in range(0, 16, 8):
            for g in range(g0, g0 + 8):
                tr(psb[0:32, (g % 8) * 128:(g % 8) * 128 + 128], Mre[:, g, :], ident[:], ["Mre", "ident"], ["psb"])
            actv(BTre[:, g0:g0 + 8, :], psb[0:32, :].rearrange("p (g s) -> p g s", s=128), AF.Copy, ["psb"], ["BTre"])
        build_M("cre", XI, "cim", XR, ALU.add, Mim, "Mim")
        for g0 in range(0, 16, 8):
            for g in range(g0, g0 + 8):
                tr(psb[0:32, (g % 8) * 128:(g % 8) * 128 + 128], Mim[:, g, :], ident[:], ["Mim", "ident"], ["psb"])
            actv(BTim[:, g0:g0 + 8, :], psb[0:32, :].rearrange("p (g s) -> p g s", s=128), AF.Copy, ["psb"], ["BTim"])
        cp("dve", Ha[:], HCR[:], ["HCR" + sfx], ["Ha"])
        ts("dve", Hni[:], HCI[:], -1.0, None, ALU.mult, None, ["HCI" + sfx], ["Hni"])
        ts("dve", Hnr[:], HCR[:], -1.0, None, ALU.mult, None, ["HCR" + sfx], ["Hnr"])
        cp("dve", Dmb[:], Dmf[:], ["Dmf" + sfx], ["Dmb"])
        fin, fk = block_table(E_all, "E_all", 128, sm["c1"][:], sm["s1"][:], [tk("c1"), tk("s1")], False)
        cp("dve", csE[:], fin[:], [fk], ["csE"])
        sq_mult(csE2, "csE2", csE, "csE")
        sq_mult(csE4, "csE4", csE2, "csE2")
        fin, fk = block_table(Bs_all, "Bs_all", 128, sm["lre"][:], sm["lim"][:], [tk("lre"), tk("lim")], True, extra_w=["P1", "P2", "P3", "P4"])
        cp("dve", Bs_bf[:], Bs_all[:], ["Bs_all", "P1", "P2", "P3", "P4"], ["Bs_bf"])
        cp("dve", sm["u1"][:], fin[:, :, 0], [fk], [tk("u1")])
        cp("dve", sm["u2"][:], fin[:, :, 1], [fk], [tk("u2")])
        block_table(A_all, "A_all", 24, sm["u1"][:], sm["u2"][:], [tk("u1"), tk("u2")], True)

        def front(pair_, pl_):
            up_ = upb[pair_ % 2]
            upk_ = f"up{pair_ % 2}"
            ut_, utk_ = utokb[pair_ % 2], f"utok{pair_ % 2}"
            bt_, btk_ = BtTb[pair_ % 2], f"BtT{pair_ % 2}"
            dma("sp", up_[:], usscr[32 * pair_:32 * pair_ + 32, :], ["usscr"], [upk_])
            for c in range(24):
                tr(psb[:, c * 32:(c + 1) * 32], up_[0:32, c * 128:(c + 1) * 128], ident[0:32, 0:32], [upk_, "ident"], ["psb"])
            tr(psb[:, 768:896], Bs_bf[:, pl_, 0, :], ident[:], ["Bs_bf", "ident"], ["psb"])
            tr(psb[:, 896:1024], Bs_bf[:, pl_, 1, :], ident[:], ["Bs_bf", "ident"], ["psb"])
            actv(ut_[:].rearrange("p c i -> p (c i)"), psb[:, 0:768], AF.Copy, ["psb"], [utk_])
            actv(bt_[:].rearrange("p a i -> p (a i)"), psb[:, 768:1024], AF.Copy, ["psb"], [btk_])
            uf_ = ut_[:].rearrange("p c i -> p (c i)")
            for a_ in range(2):
                mm(ps[2 * a_][:, 0:512], bt_[:, a_, :], uf_[:, 0:512], True, True, [btk_, utk_], [f"ps{2 * a_}"])
                mm(ps[2 * a_ + 1][:, 0:256], bt_[:, a_, :], uf_[:, 512:768], True, True, [btk_, utk_], [f"ps{2 * a_ + 1}"])

        front(16 * B, 0)
        for pl in range(16):
            pair = 16 * B + pl
            up = upb[pair % 2]
            upk = f"up{pair % 2}"
            TB = TBb[0]
            TBk = "TB0"
            gen_tables(pair, pl, sm, tk)
            slots = [per_pair // 5 + (1 if i_ >= 5 - per_pair % 5 else 0) for i_ in range(5)]
            Are, Aim = A_all[:, pl, 0, :], A_all[:, pl, 1, :]

            def abc(a, c0, c1_):
                return a[:, c0:c1_].unsqueeze(2).to_broadcast([128, c1_ - c0, 32])

            proj_slot(slots[0])
            for (Pd, pdk, glo, ghi, gk0, gk1, av) in [(P1, "P1", ps[0], ps[1], "ps0", "ps1", Are), (P4, "P4", ps[0], ps[1], "ps0", "ps1", Aim),
                                                       (P2, "P2", ps[2], ps[3], "ps2", "ps3", Aim), (P3, "P3", ps[2], ps[3], "ps2", "ps3", Are)]:
                tt("dve", Pd[:, 0:16, :], glo[:, 0:512].rearrange("p (c i) -> p c i", i=32), abc(av, 0, 16), ALU.mult, [gk0, "A_all"], [pdk])
                tt("dve", Pd[:, 16:24, :], ghi[:, 0:256].rearrange("p (c i) -> p c i", i=32), abc(av, 16, 24), ALU.mult, [gk1, "A_all"], [pdk])
            tt("dve", P1[:], P1[:], P2[:], ALU.subtract, ["P1", "P2"], ["P1"])
            tt("dve", P3[:], P3[:], P4[:], ALU.add, ["P3", "P4"], ["P3"])
            S.add("dve", (lambda o, i: (lambda e: e.tensor_reduce(out=o, in_=i, axis=AX.X, op=ALU.add)))(Xc[:, 0, :], P1[:].rearrange("p c i -> p i c")), ["P1"], ["Xc"])
            S.add("dve", (lambda o, i: (lambda e: e.tensor_reduce(out=o, in_=i, axis=AX.X, op=ALU.add)))(Xc[:, 1, :], P3[:].rearrange("p c i -> p i c")), ["P3"], ["Xc"])
            tt("dve", q1t[:], Mre[:, pl, :], Xc[:, 0, :], ALU.mult, ["Mre", "Xc"], ["q1t"])
            tt("dve", q2t[:], Mim[:, pl, :], Xc[:, 1, :], ALU.mult, ["Mim", "Xc"], ["q2t"])
            tt("dve", q1t[:], q1t[:], q2t[:], ALU.subtract, ["q1t", "q2t"], ["q1t"])
            S.add("dve", (lambda o, i: (lambda e: e.tensor_reduce(out=o, in_=i, axis=AX.X, op=ALU.add)))(xs[:, 0:1], q1t[:]), ["q1t"], ["xs"])
            tt("dve", q1t[:], Mre[:, pl, :], Xc[:, 1, :], ALU.mult, ["Mre", "Xc"], ["q1t"])
            tt("dve", q2t[:], Mim[:, pl, :], Xc[:, 0, :], ALU.mult, ["Mim", "Xc"], ["q2t"])
            tt("dve", q1t[:], q1t[:], q2t[:], ALU.add, ["q1t", "q2t"], ["q1t"])
            S.add("dve", (lambda o, i: (lambda e: e.tensor_reduce(out=o, in_=i, axis=AX.X, op=ALU.add)))(xs[:, 1:2], q1t[:]), ["q1t"], ["xs"])
            c1p, s1p = sm["c1"][:, pl:pl + 1], sm["s1"][:, pl:pl + 1]
            ts("dve", xs[:, 2:3], xs[:, 1:2], s1p, None, ALU.mult, None, ["xs", tk("s1")], ["xs2"])
            stt("dve", zi[:, 0:1], xs[:, 0:1], c1p, xs[:, 2:3], ALU.mult, ALU.subtract, ["xs", "xs2", tk("c1")], ["zi"])
            ts("dve", xs[:, 3:4], xs[:, 0:1], s1p, None, ALU.mult, None, ["xs", tk("s1")], ["xs3"])
            stt("dve", zi[:, 1:2], xs[:, 1:2], c1p, xs[:, 3:4], ALU.mult, ALU.add, ["xs", "xs3", tk("c1")], ["zi"])
            for T in range(2):
                sl = slice(T * 512, (T + 1) * 512)
                gsl = slice(SEQ - NOWN + T * 512, SEQ - NOWN + (T + 1) * 512)
                pa, pb_ = ps[2 * T], ps[2 * T + 1]
                ka, kb = f"ps{2 * T}", f"ps{2 * T + 1}"
                mm(pa[:, :], BTre[0:32, pl, :], up[0:32, gsl], True, True, ["BTre", upk], [ka])
                mm(pb_[:, :], BTim[0:32, pl, :], up[0:32, gsl], True, True, ["BTim", upk], [kb])
                proj_slot(slots[1 + T])
                if T == 1 and pl < 15:
                    pass
                tt("dve", t1[:], pa[:, :], TB[:, 0, sl], ALU.mult, [ka, TBk], ["t1"])
                tt("dve", t2[:], pb_[:, :], TB[:, 1, sl], ALU.mult, [kb, TBk], ["t2"])
                tt("dve", wre[:, sl], t1[:], t2[:], ALU.add, ["t1", "t2"], ["wre"])
                tt("dve", t1[:], pb_[:, :], TB[:, 0, sl], ALU.mult, [kb, TBk], ["t1"])
                tt("dve", t2[:], pa[:, :], TB[:, 1, sl], ALU.mult, [ka, TBk], ["t2"])
                tt("dve", wim[:, sl], t1[:], t2[:], ALU.subtract, ["t1", "t2"], ["wim"])
            for (wsrc, wkey, Zc_, zck, Zs_, zsk, zcol) in [(wre, "wre", Zcr, "Zcr", Zsr, "Zsr", 0), (wim, "wim", Zci, "Zci", Zsi, "Zsi", 1)]:
                S.add("dve", (lambda zz, mg, ww, ini: (lambda e: e.tensor_tensor_scan(out=zz, data0=mg, data1=ww, initial=ini, op0=ALU.mult, op1=ALU.add)))(
                    zbuf[:], sm["mag"][:, pl:pl + 1].to_broadcast([128, NOWN]), wsrc[:], zi[:, zcol:zcol + 1]), [tk("mag"), wkey, "zi"], ["zbuf"])
                tt("dve", Zc_[:], TB[:, 0, :], zbuf[:], ALU.mult, [TBk, "zbuf"], [zck])
                tt("dve", Zs_[:], TB[:, 1, :], zbuf[:], ALU.mult, [TBk, "zbuf"], [zsk])
            if pl < 15:
                front(pair + 1, pl + 1)
                proj_slot(slots[3])
            for q in range(8):
                o = ps[4][:, q * 32:(q + 1) * 32]
                qs_ = slice(q * 128, (q + 1) * 128)
                mm(o, Zcr[:, qs_], Ha[:, pl, :], True, False, ["Zcr", "Ha"], ["ps4"])
                mm(o, Zci[:, qs_], Hni[:, pl, :], False, False, ["Zci", "Hni"], ["ps4"])
                mm(o, Zsr[:, qs_], Hni[:, pl, :], False, False, ["Zsr", "Hni"], ["ps4"])
                mm(o, Zsi[:, qs_], Hnr[:, pl, :], False, False, ["Zsi", "Hnr"], ["ps4"])
                mm(o, up[0:32, SEQ - NOWN + q * 128:SEQ - NOWN + (q + 1) * 128], Dmb[0:32, pl, :], False, True, [upk, "Dmb"], ["ps4"])
            proj_slot(slots[4])
            actv(Y[:, :, 32 * pair:32 * pair + 32], ps[4][:, 0:256].rearrange("p (q c) -> p q c", c=32), AF.Copy, ["ps4"], ["Y"])
        proj_slot(len(pq))
    if DEBUG:
        dma("sp", dbg["yssm"], Y[:], ["Y"], ["dbg_yssm"])
        dbg_written.append("dbg_yssm")
    if stop == "s4":
        return finish()
    barrier()
    A.off = ssm_base
    BsT = A.alloc("BsT", [128, 16, NOWN], BF16)
    g1 = A.alloc("g1", [128, D], F32)
    for q in range(8):
        tt("dve", g1[:], Y[:, q, :], Y[:, q, :], ALU.mult, ["Y"], ["g1"])
        ts("dve", g1[:], g1[:], 0.044715, 1.0, ALU.mult, ALU.add, ["g1"], ["g1"])
        tt("dve", g1[:], g1[:], Y[:, q, :], ALU.mult, ["g1", "Y"], ["g1"])
        actv(g1[:], g1[:], AF.Sigmoid, ["g1"], ["g1"], scale=1.5957691216)
        tt("dve", Y[:, q, :], Y[:, q, :], g1[:], ALU.mult, ["Y", "g1"], ["Y"])
        for f0 in range(0, 16, 4):
            hk = "psb" if (f0 // 4) % 2 == 0 else "psb"
            pv = psb[:, ((f0 // 4) % 2) * 512:((f0 // 4) % 2) * 512 + 512]
            for f in range(f0, f0 + 4):
                tr(pv[:, (f - f0) * 128:(f - f0 + 1) * 128], Y[:, q, f * 128:(f + 1) * 128], ident[:], ["Y", "ident"], [hk])
            actv(BsT[:, f0:f0 + 4, q * 128:(q + 1) * 128], pv.rearrange("p (f t) -> p f t", t=128), AF.Copy, [hk], ["BsT"])
    bsscr3 = bsscr.rearrange("c p t -> p c t")
    dma("sp", bsscr3, BsT[:], ["BsT"], ["bsscr"])
    dump("BsT", BsT[:], [128, 16, NOWN], BF16, ["BsT"])
    barrier()
    A.off = persist

    AT = A.alloc("AT", [128, 16, NOWN], BF16)
    w1k = A.alloc("w1k", [128, 32, 128], BF16)
    w1v = w1k
    w2k = A.alloc("w2k", [128, 128], BF16)
    w2v = A.alloc("w2v", [128, 128], BF16)
    dma("pool", w2k[:], d_wck2, (), ["w2k"])
    dma("pool", w2v[:], d_wcv2, (), ["w2v"])
    pekf = load("pekf", [128, 32], d_pek)
    pevf = load("pevf", [128, 32], d_pev)
    pekb = A.alloc("pekb", [128, 32], BF16)
    pevb = A.alloc("pevb", [128, 32], BF16)
    cp("dve", pekb[:], pekf[:], ["pekf"], ["pekb"])
    cp("dve", pevb[:], pevf[:], ["pevf"], ["pevb"])
    ovf = load("ovf", [128, 2, 64], d_ov)
    ovb = A.alloc("ovb", [128, 2, 64], BF16)
    cp("dve", ovb[:], ovf[:], ["ovf"], ["ovb"])
    Esel = A.alloc("Esel", [64, 32, 128], BF16)
    for kc4 in range(0, 32, 8):
        dma("pool", Esel[:, kc4:kc4 + 8, :], d_E[:, kc4:kc4 + 8, :], (), ["Esel"])
    tri4 = A.alloc("tri4", [128, 128], BF16)
    tris4 = A.alloc("tris4", [128, 128], BF16)
    mc4 = A.alloc("mc4", [128, 8, 128], BF16)
    dma("pool", tri4[:], d_tri, (), ["tri4"])
    dma("pool", tris4[:], d_tris, (), ["tris4"])
    dma("pool", mc4[:], d_mc4, (), ["mc4"])
    cmpb = load("cmpb", [128, 2], d_cmpb)
    winb = load("winb", [128, 12], d_winb)
    scmul = load("scmul", [128, 8, 64], d_scmul)
    scadd = load("scadd", [128, 8, 64], d_scadd)
    scval = load("scval", [128, 8, 64], d_scval)
    KcT = A.alloc("KcT", [128, SEQ], BF16)
    VcT = A.alloc("VcT", [128, SEQ], BF16)
    KsT = A.alloc("KsT", [128, SEQ], BF16)
    KwT = A.alloc("KwT", [128, 1536], BF16)
    Vs = A.alloc("Vs", [128, 32, 130], BF16)
    Vw = A.alloc("Vw", [128, 12, 130], BF16)
    QT = A.alloc("QT", [128, 8, 4, 128], BF16)
    Gt = A.alloc("Gt", [128, 8, 12], F32)
    kcT = A.alloc("kcT", [128, 256], BF16)
    vcm = A.alloc("vcm", [128, 2, 130], BF16)
    hidT = A.alloc("hidT", [128, 256], BF16)
    hb = A.alloc("hb", [128, 256], F32)
    hg = A.alloc("hg", [128, 256], F32)
    cbias = A.alloc("cbias", [128, 2], F32)
    Pc = A.alloc("Pc", [128, 2, 512], BF16)
    Pb = [A.alloc("Pb", [128, 512], BF16) for _ in range(2)]
    selT4 = A.alloc("selT4", [64, 4, 128], BF16)
    selneg = A.alloc("selneg", [128, 64], BF16)
    sc = A.alloc("sc", [128, 64], F32)
    sc2 = A.alloc("sc2", [128, 64], F32)
    m8 = A.alloc("m8", [128, 8], F32)
    thr = A.alloc("thr", [128, 1], F32)
    rden = A.alloc("rden", [128, 4], F32)
    coef = A.alloc("coef", [128, 4], F32)
    yat = A.alloc("yat", [128, 4, 128], F32)
    yab = A.alloc("yab", [128, 4, 128], BF16)
    ms("dve", Vs[:, :, 128:130], 1.0, ["Vs"])
    ms("dve", Vw[:, :, 128:130], 1.0, ["Vw"])
    ms("dve", vcm[:, :, 128:130], 1.0, ["vcm"])
    ms("dve", hidT[:], 0.0, ["hidT"])
    ms("dve", kcT[:], 0.0, ["kcT"])
    def gelu_evac(psrc, pkey, bias_ap, bkey):
        actv(hb[:, 0:255], psrc, AF.Identity, [pkey, bkey], ["hb"], bias=bias_ap)
        tt("dve", hg[:, 0:255], hb[:, 0:255], hb[:, 0:255], ALU.mult, ["hb"], ["hg"])
        ts("dve", hg[:, 0:255], hg[:, 0:255], 0.044715, 1.0, ALU.mult, ALU.add, ["hg"], ["hg"])
        tt("dve", hg[:, 0:255], hg[:, 0:255], hb[:, 0:255], ALU.mult, ["hg", "hb"], ["hg"])
        actv(hg[:, 0:255], hg[:, 0:255], AF.Sigmoid, ["hg"], ["hg"], scale=1.5957691216)
        tt("dve", hidT[:, 0:255], hb[:, 0:255], hg[:, 0:255], ALU.mult, ["hb", "hg"], ["hidT"])

    for g in range(4):
        gk = [f"kscr{g}"]
        dma("sp", KcT[:], kscr[g, 0], gk, ["KcT"])
        dma("sp", VcT[:], kscr[g, 1], gk, ["VcT"])
        dma("sp", KsT[:], kscr[g, 2], gk, ["KsT"])
        dma("sp", KwT[:], kwscr[g], gk, ["KwT"])
        dma("sp", Vs[:, :, 0:128], vsscr[g], gk, ["Vs"])
        dma("sp", Vw[:, :, 0:128], vwscr[g], gk, ["Vw"])
        dma("sp", QT[:], qscr[g], gk, ["QT"])
        dma("sp", Gt[:], gscr[g], gk, ["Gt"])
        for (w1, w1n, peb, pebn, src, srcn, w2, w2n, isk) in [(w1k, "w1k", pekb, "pekb", KcT, "KcT", w2k, "w2k", True),
                                                          (w1v, "w1v", pevb, "pevb", VcT, "VcT", w2v, "w2v", False)]:
            dma("pool", w1[:], (d_wck1 if isk else d_wcv1).rearrange("l d e -> d l e"), (), ["w1k"])
            w1n = "w1k"
            for l in range(32):
                mm(ps[5][:, 0:1], w1[:, l, :], peb[:, l:l + 1], l == 0, l == 31, [w1n, pebn], ["ps5"])
            cp("dve", cbias[:, 0:1], ps[5][:, 0:1], ["ps5"], ["cbias"])
            for l in range(32):
                mm(ps[6][:, 0:255], w1[:, l, :], (src[:].rearrange("p (n s) -> p n s", s=16)[:, 0:255, l] if l < 16 else src[:].rearrange("p (n s) -> p n s", s=16)[:, 1:256, l - 16]), l == 0, l == 31, [w1n, srcn], ["ps6"])
            gelu_evac(ps[6][:, 0:255], "ps6", cbias[:, 0:1], "cbias")
            if isk:
                mm(ps[5][:, 0:256], w2[:, :], hidT[:, :], True, True, [w2n, "hidT"], ["ps5"])
                actv(kcT[:, 0:255], ps[5][:, 0:255], AF.Copy, ["ps5"], ["kcT"])
            else:
                for c in range(2):
                    mm(ps[5][:, c * 128:(c + 1) * 128], hidT[:, c * 128:(c + 1) * 128], w2[:, :], True, True, [w2n, "hidT"], ["ps5"])
                actv(vcm[:, :, 0:128], ps[5][:, 0:256].rearrange("p (c f) -> p c f", f=128), AF.Copy, ["ps5"], ["vcm"])
        for qi in range(8):
            Qt = QT[:, qi, :, :].rearrange("p h t -> p (h t)")
            OB = [2, 3, 5, 6]

            def Oh(h):
                return ps[OB[h]][:, 0:130]

            def Ok(h):
                return f"ps{OB[h]}"

            def finish_branch(br, first):
                for h in range(4):
                    ts("dve", rden[:, h:h + 1], Oh(h)[:, 128:129], 1e-30, None, ALU.max, None, [Ok(h)], ["rden"])
                S.add("dve", (lambda o: (lambda e: e.reciprocal(out=o, in_=o)))(rden[:]), ["rden"], ["rden"])
                tt("dve", coef[:], rden[:], Gt[:, qi, :].rearrange("p (h b) -> p h b", b=3)[:, :, br], ALU.mult, ["rden", "Gt"], ["coef"])
                for h in range(4):
                    if first:
                        ts("dve", yat[:, h, :], Oh(h)[:, 0:128], coef[:, h:h + 1], None, ALU.mult, None, [Ok(h), "coef"], ["yat"])
                    else:
                        stt("dve", yat[:, h, :], Oh(h)[:, 0:128], coef[:, h:h + 1], yat[:, h, :], ALU.mult, ALU.add, [Ok(h), "coef", "yat"], ["yat"])

            for c in range(2):
                mm(ps[c][:, :], kcT[:, c * 128:(c + 1) * 128], Qt, True, True, ["kcT", "QT"], [f"ps{c}"])
                actv(Pc[:, c, :], ps[c][:, :], AF.Exp, [f"ps{c}", "cmpb"], ["Pc"], bias=cmpb[:, c:c + 1])
            tt("dve", Pc[:, 1, :].rearrange("p (h t) -> p h t", t=128), Pc[:, 1, :].rearrange("p (h t) -> p h t", t=128), mc4[:, qi, :].unsqueeze(1).to_broadcast([128, 4, 128]), ALU.mult, ["Pc", "mc4"], ["Pc"])
            for h in range(4):
                for c in range(2):
                    mm(Oh(h)[:, 0:129], Pc[:, c, h * 128:(h + 1) * 128], vcm[:, c, 0:129], c == 0, c == 1, ["Pc", "vcm"], [Ok(h)])
                    mm(ps[4][:, h * 64:(h + 1) * 64], Pc[:, c, h * 128:(h + 1) * 128], ovb[:, c, :], c == 0, c == 1, ["Pc", "ovb"], ["ps4"])
            finish_branch(0, True)
            for h in range(4):
                if h == 0:
                    ts("dve", sc[:], ps[4][:, 0:64], rden[:, 0:1], None, ALU.mult, None, ["ps4", "rden"], ["sc"])
                else:
                    stt("dve", sc[:], ps[4][:, h * 64:(h + 1) * 64], rden[:, h:h + 1], sc[:], ALU.mult, ALU.add, ["ps4", "rden", "sc"], ["sc"])
            tt("dve", sc[:], sc[:], scmul[:, qi, :], ALU.mult, ["sc", "scmul"], ["sc"])
            tt("dve", sc[:], sc[:], scadd[:, qi, :], ALU.add, ["sc", "scadd"], ["sc"])
            S.add("dve", lambda e: e.max(out=m8[:], in_=sc[:]), ["sc"], ["m8"])
            S.add("dve", lambda e: e.match_replace(out=sc2[:], in_to_replace=m8[:], in_values=sc[:], imm_value=-3.0e6), ["sc", "m8"], ["sc2"])
            S.add("dve", lambda e: e.max(out=m8[:], in_=sc2[:]), ["sc2"], ["m8"])
            S.add("dve", lambda e: e.tensor_reduce(out=thr[:], in_=m8[:], axis=AX.X, op=ALU.min), ["m8"], ["thr"])
            ts("dve", sc2[:], sc[:], thr[:, 0:1], None, ALU.is_ge, None, ["sc", "thr"], ["sc2"])
            tt("dve", sc2[:], sc2[:], scval[:, qi, :], ALU.mult, ["sc2", "scval"], ["sc2"])
            ts("dve", selneg[:], sc2[:], -1.0, -NEGM, ALU.add, ALU.mult, ["sc2"], ["selneg"])
            tr(psb[0:64, 0:128], selneg[:, :], ident[:], ["selneg", "ident"], ["psb"])
            for h in range(4):
                cp("dve", selT4[:, h, :], psb[0:64, 0:128], ["psb"], ["selT4"])
            selflat = selT4[:].rearrange("p h t -> p (h t)")
            nk = 24 + qi + 1

            def pv_sel(kc):
                P_ = Pb[kc % 2]
                Pk = f"Pb{kc % 2}"
                for h in range(4):
                    mm(Oh(h)[:, 0:129], P_[:, h * 128:(h + 1) * 128], Vs[:, kc, 0:129], kc == 0, kc == nk - 1, [Pk, "Vs"], [Ok(h)])

            pend = None
            for kc in range(nk):
                pS = ps[kc % 2]
                pk = f"ps{kc % 2}"
                mm(pS[:, :], KsT[:, kc * 128:(kc + 1) * 128], Qt, True, False, ["KsT", "QT"], [pk])
                mm(pS[:, :], Esel[:, kc, :], selflat, False, True, ["Esel", "selT4"], [pk])
                P_ = Pb[kc % 2]
                Pk = f"Pb{kc % 2}"
                actv(P_[:], pS[:, :], AF.Exp, [pk], [Pk])
                if kc == nk - 1:
                    tt("dve", P_[:].rearrange("p (h t) -> p h t", t=128), P_[:].rearrange("p (h t) -> p h t", t=128), tri4[:].unsqueeze(1).to_broadcast([128, 4, 128]), ALU.mult, [Pk, "tri4"], [Pk])
                if pend is not None:
                    pv_sel(pend)
                pend = kc
            pv_sel(pend)
            finish_branch(1, False)
            def pv_win(wi):
                P_ = Pb[wi % 2]
                Pk = f"Pb{wi % 2}"
                kl = qi + wi
                for h in range(4):
                    mm(Oh(h)[:, 0:129], P_[:, h * 128:(h + 1) * 128], Vw[:, kl, 0:129], wi == 0, wi == 4, [Pk, "Vw"], [Ok(h)])

            pend = None
            for wi in range(5):
                kc = 20 + qi + wi
                kl = kc - 20
                pS = ps[wi % 2]
                pk = f"ps{wi % 2}"
                mm(pS[:, :], KwT[:, kl * 128:(kl + 1) * 128], Qt, True, True, ["KwT", "QT"], [pk])
                P_ = Pb[wi % 2]
                Pk = f"Pb{wi % 2}"
                actv(P_[:], pS[:, :], AF.Exp, [pk, "winb"], [Pk], bias=winb[:, kl:kl + 1])
                if wi == 0:
                    tt("dve", P_[:].rearrange("p (h t) -> p h t", t=128), P_[:].rearrange("p (h t) -> p h t", t=128), tris4[:].unsqueeze(1).to_broadcast([128, 4, 128]), ALU.mult, [Pk, "tris4"], [Pk])
                if wi == 4:
                    tt("dve", P_[:].rearrange("p (h t) -> p h t", t=128), P_[:].rearrange("p (h t) -> p h t", t=128), tri4[:].unsqueeze(1).to_broadcast([128, 4, 128]), ALU.mult, [Pk, "tri4"], [Pk])
                if pend is not None:
                    pv_win(pend)
                pend = wi
            pv_win(pend)
            finish_branch(2, False)
            cp("dve", yab[:], yat[:], ["yat"], ["yab"])
            for h in range(4):
                tr(psb[:, 512 + h * 128:512 + (h + 1) * 128], yab[:, h, :], ident[:], ["yab", "ident"], ["psb"])
            actv(AT[:, 4 * g:4 * g + 4, qi * 128:(qi + 1) * 128], psb[:, 512:1024].rearrange("p (h t) -> p h t", t=128), AF.Copy, ["psb"], ["AT"])
    if DEBUG:
        dma("sp", dbg["yaT"], AT[:], ["AT"], ["dbg_yaT"])
        dbg_written.append("dbg_yaT")
    if stop == "s5":
        return finish()
    barrier()
    at_end = persist + 16 * NOWN * 2
    A.off = at_end

    mT = A.alloc("mT", [128, 16, NOWN], BF16)
    mt_end = A.off
    BsT = A.alloc("BsT2", [128, 16, NOWN], BF16)
    dma("sp", BsT[:], bsscr3, ["bsscr"], ["BsT"])
    uTo = A.alloc("uTo", [128, 16, NOWN], BF16)
    dma("sp", uTo[:], uscr3[:, :, SEQ - NOWN:SEQ], ["uscr"], ["uTo"])
    wblk = [A.alloc("wblk", [128, 16, 512], BF16) for _ in range(3)]
    e1 = A.alloc("e1", [128, 512], F32)
    e2 = A.alloc("e2", [128, 512], F32)
    e3 = A.alloc("e3", [128, 512], F32)
    wi_ = 0

    def wload(src3, c0, ncols=512):
        nonlocal wi_
        w_ = wblk[wi_ % len(wblk)]
        wk = f"wblk{wi_ % len(wblk)}"
        wi_ += 1
        dma("pool", w_[:, :, 0:ncols], src3[:, :, c0:c0 + ncols], (), [wk])
        return w_, wk

    d_wglu3 = d_wglu.rearrange("(k p) n -> p k n", p=128)
    d_wout3 = d_wout.rearrange("(k p) n -> p k n", p=128)
    d_wup3 = d_wup.rearrange("(k p) n -> p k n", p=128)
    for nb in range(4):
        wg, wgk = wload(d_wglu3, nb * 512)
        wa_, wak = wload(d_win3, 7216 + nb * 512)
        wb_, wbk = wload(d_win3, 9264 + nb * 512)
        for cc in range(4):
            n = nb * 4 + cc
            for th in range(2):
                tsl = slice(th * 512, (th + 1) * 512)
                for k in range(16):
                    mm(ps[0][:, :], wg[:, k, cc * 128:(cc + 1) * 128], BsT[:, k, tsl], k == 0, k == 15, [wgk, "BsT"], ["ps0"])
                actv(e1[:], ps[0][:, :], AF.Sigmoid, ["ps0", "bglu"], ["e1"], bias=bglu[:, n:n + 1])
                for k in range(16):
                    mm(ps[1][:, :], wa_[:, k, cc * 128:(cc + 1) * 128], uTo[:, k, tsl], k == 0, k == 15, [wak, "uTo"], ["ps1"])
                actv(e2[:], ps[1][:, :], AF.Sigmoid, ["ps1"], ["e2"])
                for k in range(16):
                    mm(ps[2][:, :], wb_[:, k, cc * 128:(cc + 1) * 128], uTo[:, k, tsl], k == 0, k == 15, [wbk, "uTo"], ["ps2"])
                actv(e3[:], ps[2][:, :], AF.Sigmoid, ["ps2"], ["e3"])
                tt("dve", e1[:], e1[:], BsT[:, n, tsl], ALU.mult, ["e1", "BsT"], ["e1"])
                tt("dve", e1[:], e1[:], e3[:], ALU.mult, ["e1", "e3"], ["e1"])
                tt("dve", e2[:], e2[:], AT[:, n, tsl], ALU.mult, ["e2", "AT"], ["e2"])
                tt("dve", mT[:, n, tsl], e1[:], e2[:], ALU.add, ["e1", "e2"], ["mT"])
    barrier()
    A.off = mt_end
    mT2 = mT
    h1T = A.alloc("h1T", [128, 16, NOWN], F32)
    sq = A.alloc("sq", [128, 512], F32)
    rsb = A.alloc("rsb", [128, NOWN], F32)
    rs_end = A.off
    xo = A.alloc("xo", [128, D], F32)
    wblk = [A.alloc("wblk", [128, 16, 512], BF16) for _ in range(2)]
    for q in range(8):
        dma("sp", xo[:], d_xw[SEQ - NOWN + q * 128:SEQ - NOWN + (q + 1) * 128, :], (), ["xo"])
        for f0 in range(0, 16, 4):
            for f in range(f0, f0 + 4):
                tr(ps[4][:, (f - f0) * 128:(f - f0 + 1) * 128], xo[:, f * 128:(f + 1) * 128], identf[:], ["xo", "identf"], ["ps4"])
            actv(h1T[:, f0:f0 + 4, q * 128:(q + 1) * 128], ps[4][:, :].rearrange("p (f t) -> p f t", t=128), AF.Copy, ["ps4"], ["h1T"])
    for nb in range(4):
        wo, wok = wload(d_wout3, nb * 512)
        for cc in range(4):
            n = nb * 4 + cc
            for th in range(2):
                tsl = slice(th * 512, (th + 1) * 512)
                for k in range(16):
                    mm(ps[th][:, :], wo[:, k, cc * 128:(cc + 1) * 128], mT2[:, k, tsl], k == 0, k == 15, [wok, "mT"], [f"ps{th}"])
                stt("dve", h1T[:, n, tsl], ps[th][:, :], ada[:, 32 + n:33 + n], h1T[:, n, tsl], ALU.mult, ALU.add, [f"ps{th}", "ada", "h1T"], ["h1T"])
    def feat_rstd(srcT, skey):
        for th in range(2):
            tsl = slice(th * 512, (th + 1) * 512)
            for n in range(16):
                actv(sq[:], srcT[:, n, tsl], AF.Square, [skey], ["sq"])
                mm(ps[3][:, :], onesf[:, :], sq[:], n == 0, n == 15, ["onesf", "sq"], ["ps3"])
            ts("dve", rsb[:, tsl], ps[3][:, :], 1.0 / D, EPS, ALU.mult, ALU.add, ["ps3"], ["rsb"])
            actv(rsb[:, tsl], rsb[:, tsl], AF.Sqrt, ["rsb"], ["rsb"])
            S.add("dve", (lambda o: (lambda e: e.reciprocal(out=o, in_=o)))(rsb[:, tsl]), ["rsb"], ["rsb"])

    dump("h1T", h1T[:], [128, 16, NOWN], F32, ["h1T"])
    dump("mT", mT[:], [128, 16, NOWN], BF16, ["mT"])
    feat_rstd(h1T, "h1T")
    u2T = AT
    barrier()
    for n in range(16):
        for th in range(2):
            tsl = slice(th * 512, (th + 1) * 512)
            tt("dve", sq[:], h1T[:, n, tsl], rsb[:, tsl], ALU.mult, ["h1T", "rsb"], ["sq"])
            actv(u2T[:, n, tsl], sq[:], AF.Identity, ["sq", "m2s", "ada"], ["u2T"], bias=ada[:, 48 + n:49 + n], scale=m2s[:, n:n + 1])
    dump("u2T", u2T[:], [128, 16, NOWN], BF16, ["u2T"])
    h1scr3 = h1scr.rearrange("c p t -> p c t")
    dma("sp", h1scr3, h1T[:], ["h1T"], ["h1scr"])
    barrier()
    acc = h1T
    A.off = at_end
    wdb = [A.alloc("wdb", [128, 4, D], BF16) for _ in range(2)]
    assert A.off <= mt_end
    A.off = rs_end
    hid = A.alloc("hid", [128, 4, NOWN], BF16)
    rl = [A.alloc("rl", [128, 512], F32) for _ in range(2)]
    wblk = [A.alloc("wblk", [128, 16, 512], BF16) for _ in range(2)]
    d_wdn3 = d_wdn.rearrange("(k p) n -> p k n", p=128)
    for fb in range(16):
        wu, wuk = wload(d_wup3, fb * 512)
        wd = wdb[fb % 2]
        wdk = f"wdb{fb % 2}"
        dma("pool", wd[:], d_wdn3[:, fb * 4:fb * 4 + 4, :], (), [wdk])
        for cc in range(4):
            for th in range(2):
                tsl = slice(th * 512, (th + 1) * 512)
                p_ = ps[(cc * 2 + th) % 2]
                pk = f"ps{(cc * 2 + th) % 2}"
                for k in range(16):
                    mm(p_[:, :], wu[:, k, cc * 128:(cc + 1) * 128], u2T[:, k, tsl], k == 0, k == 15, [wuk, "u2T"], [pk])
                rl_ = rl[(cc * 2 + th) % 2]
                rk = f"rl{(cc * 2 + th) % 2}"
                actv(rl_[:], p_[:, :], AF.Relu, [pk], [rk])
                tt("dve", hid[:, cc, tsl], rl_[:], rl_[:], ALU.mult, [rk], ["hid"])
        for n in range(16):
            for th in range(2):
                tsl = slice(th * 512, (th + 1) * 512)
                p_ = ps[2 + (n * 2 + th) % 2]
                pk = f"ps{2 + (n * 2 + th) % 2}"
                for k in range(4):
                    mm(p_[:, :], wd[:, k, n * 128:(n + 1) * 128], hid[:, k, tsl], k == 0, k == 3, [wdk, "hid"], [pk])
                if fb == 0:
                    cp("dve", acc[:, n, tsl], p_[:, :], [pk], ["acc"])
                else:
                    tt("dve", acc[:, n, tsl], acc[:, n, tsl], p_[:, :], ALU.add, [pk, "acc"], ["acc"])
    barrier()
    A.off = rs_end
    hch = [A.alloc("hch", [128, NOWN], F32) for _ in range(2)]
    for n in range(16):
        hc = hch[n % 2]
        hk = f"hch{n % 2}"
        dma("sp", hc[:], h1scr3[:, n, :], ["h1scr"], [hk])
        stt("dve", acc[:, n, :], acc[:, n, :], ada[:, 80 + n:81 + n], hc[:], ALU.mult, ALU.add, ["acc", "ada", hk], ["acc"])
    dump("h2T", acc[:], [128, 16, NOWN], F32, ["acc"])
    feat_rstd(acc, "acc")
    ot = A.alloc("ot", [128, 16, 128], F32)
    orow = [A.alloc("orow", [128, D], F32) for _ in range(2)]
    for q in range(8):
        qs = slice(q * 128, (q + 1) * 128)
        for n in range(16):
            stt("dve", ot[:, n, :], acc[:, n, qs], gfin[:, n:n + 1], rsb[:, qs], ALU.mult, ALU.mult, ["acc", "gfin", "rsb"], ["ot"])
        orw = orow[q % 2]
        ok_ = f"orow{q % 2}"
        for f0 in range(0, 16, 4):
            for f in range(f0, f0 + 4):
                tr(ps[4][:, (f - f0) * 128:(f - f0 + 1) * 128], ot[:, f, :], identf[:], ["ot", "identf"], ["ps4"])
            actv(orw[:, f0 * 128:(f0 + 4) * 128], ps[4][:, :], AF.Copy, ["ps4"], [ok_])
        dma("sp", d_out[qs, :], orw[:], [ok_], [f"out{q}"])
    return finish()


def _prep(inputs):
    f32 = np.float32
    x = np.asarray(inputs["x"], f32)
    c = np.asarray(inputs["c"], f32)

    def col(v, n):
        return np.ascontiguousarray(np.asarray(v, f32).reshape(n, 128).T)

    a_re = np.asarray(inputs["a_re"], f32)[0]
    a_im = np.asarray(inputs["a_im"], f32)[0]
    log_dt = np.asarray(inputs["log_dt"], f32)[0]
    b_re = np.asarray(inputs["b_re"], f32)[0]
    b_im = np.asarray(inputs["b_im"], f32)[0]
    c_re = np.asarray(inputs["c_re"], f32)[0]
    c_im = np.asarray(inputs["c_im"], f32)[0]
    d_skip = np.asarray(inputs["d_skip"], f32)[0]
    def pairlay(v):
        sh = v.shape
        return np.ascontiguousarray(v.reshape((64, 2) + sh[1:]).transpose((1, 2, 0) + tuple(range(3, len(sh) + 1))).reshape((128, 64) + sh[2:]))

    HCR = np.zeros((2, 64, 64, 32), f32)
    HCI = np.zeros((2, 64, 64, 32), f32)
    Dm = np.zeros((32, 64, 32), f32)
    cr = c_re.reshape(64, 2, 16, 64)
    ci_ = c_im.reshape(64, 2, 16, 64)
    for e_ in range(2):
        HCR[e_, :, :, 16 * e_:16 * e_ + 16] = cr[:, e_].transpose(2, 0, 1)
        HCI[e_, :, :, 16 * e_:16 * e_ + 16] = ci_[:, e_].transpose(2, 0, 1)
        for cc in range(16):
            Dm[16 * e_ + cc, :, 16 * e_ + cc] = d_skip.reshape(64, 2, 16)[:, e_, cc]
    shared = {
        "w_ada": np.ascontiguousarray(inputs["w_ada"][0], dtype=f32),
        "bada": col(inputs["b_ada"][0], 96),
        "gmix": col(inputs["g_mix"][0], 16),
        "gmlp": col(inputs["g_mlp"][0], 16),
        "gfin": col(inputs["g_final"], 16),
        "bglu": col(inputs["b_glu"][0], 16),
        "w_in": np.ascontiguousarray(inputs["w_in"][0], dtype=f32),
        "w_ck1": np.ascontiguousarray(inputs["w_ck1"][0], dtype=f32),
        "w_ck2": np.ascontiguousarray(inputs["w_ck2"][0], dtype=f32),
        "pek": np.ascontiguousarray(np.asarray(inputs["pe_ck"][0], f32).T),
        "w_cv1": np.ascontiguousarray(inputs["w_cv1"][0], dtype=f32),
        "w_cv2": np.ascontiguousarray(inputs["w_cv2"][0], dtype=f32),
        "pev": np.ascontiguousarray(np.asarray(inputs["pe_cv"][0], f32).T),
        "are2": pairlay(a_re),
        "aim2": pairlay(a_im),
        "ldt2": pairlay(np.ascontiguousarray(np.broadcast_to(log_dt[:, None], (128, 64)))),
        "X1": pairlay(b_re),
        "X2": pairlay(b_im),
        "CRI": np.ascontiguousarray(HCR.reshape(128, 64, 32)),
        "CIR": np.ascontiguousarray(HCI.reshape(128, 64, 32)),
        "Dm": Dm,
        "w_glu": np.ascontiguousarray(inputs["w_glu"][0], dtype=f32),
        "w_out": np.ascontiguousarray(inputs["w_out"][0], dtype=f32),
        "w_up": np.ascontiguousarray(inputs["w_up"][0], dtype=f32),
        "w_down": np.ascontiguousarray(inputs["w_down"][0], dtype=f32),
    }
    n = np.arange(256)
    jb = np.arange(64)
    ovm = ((16 * n[:, None] < 64 * jb[None, :] + 64) & (16 * n[:, None] + 32 > 64 * jb[None, :]) & (n[:, None] <= 254)).astype(f32)
    shared["ov"] = np.ascontiguousarray(ovm.reshape(2, 128, 64).transpose(1, 0, 2))
    E = np.zeros((64, 32, 128), f32)
    for kc in range(32):
        for k in range(128):
            E[2 * kc + k // 64, kc, k] = 1.0
    shared["Esel"] = E
    kk = np.arange(128)[:, None]
    ql = np.arange(128)[None, :]
    shared["tri4"] = np.ascontiguousarray((kk <= ql).astype(f32))
    shared["tris4"] = np.ascontiguousarray((kk > ql).astype(f32))
    mc = np.zeros((128, 8, 128), f32)
    for qi in range(8):
        mc[:, qi, :] = (16 * kk - ql <= 993 + 128 * qi).astype(f32)
    shared["mc4"] = mc
    maps = []
    for core in range(8):
        b, j = core // 4, core % 4
        P = 1024 * (3 - j)
        xw = np.zeros((SEQ, D), f32)
        xw[P:] = x[b, :SEQ - P]
        m = dict(shared)
        m["xw"] = xw
        m["ccol"] = col(c[b], 16)
        tau = np.arange(SEQ)
        m["mtok"] = np.ascontiguousarray(np.broadcast_to((tau >= P).astype(f32)[None, :], (128, SEQ)))
        ncmp = np.arange(256).reshape(2, 128).T
        m["cmpb"] = np.where((ncmp <= 254) & (16 * ncmp >= P), 0.0, NEGM).astype(f32)
        kw = (128 * (20 + np.arange(12))[None, :] + np.arange(128)[:, None])
        m["winb"] = np.where(kw >= P, 0.0, NEGM).astype(f32)
        f0 = P // 64
        scmul = np.zeros((128, 8, 64), f32)
        scadd = np.zeros((128, 8, 64), f32)
        scval = np.zeros((128, 8, 64), f32)
        for qi in range(8):
            t = 3072 + 128 * qi + np.arange(128)
            cur = (t // 64)[:, None]
            jj = jb[None, :]
            valid = (jj >= f0) & (jj <= cur)
            forced = valid & ((jj == f0) | (jj == cur) | (jj == cur - 1))
            scmul[:, qi, :] = (valid & ~forced)
            scadd[:, qi, :] = np.where(forced, 1.0e6 + jj, np.where(valid, 0.0, -1.0e6 - jj))
            scval[:, qi, :] = valid
        m["scmul"], m["scadd"], m["scval"] = scmul, scadd, scval
        maps.append(m)
    return maps


_CACHE = {}


def kernel(**inputs):
    maps = _prep(inputs)
    if "nc" not in _CACHE:
        _CACHE["nc"] = build()
    nc, S = _CACHE["nc"]
    res = run_bass_kernel_spmd(nc, maps, core_ids=list(range(8)))
    out = np.zeros((2, SEQ, D), np.float32)
    for core in range(8):
        b, j = core // 4, core % 4
        out[b, 1024 * j:1024 * (j + 1)] = res.results[core]["out"]
    if DEBUG:
        _CACHE["dbg"] = res.results
    return out
```

```python
import contextlib
import numpy as np
import concourse.bass as bass
import concourse.mybir as mybir
from concourse.bass_utils import run_bass_kernel_spmd

F32 = mybir.dt.float32
BF16 = mybir.dt.bfloat16
ALU = mybir.AluOpType
AF = mybir.ActivationFunctionType
AX = mybir.AxisListType

D = 2048
SEQ = 4096
NOWN = 1024
EPS = 1e-6
NEGM = -30000.0
PI = float(np.pi)
DEBUG = False

EPOCH = 12000
NSLOT = 10


class Sched:
    def __init__(self, nc):
        self.nc = nc
        self.ops = []
        self.phase = 0

    def add(self, eng, fn, r=(), w=(), dma=False):
        self.ops.append(dict(eng=eng, fn=fn, r=tuple(r) + (("PH", self.phase),), w=tuple(w), dma=dma))

    def barrier(self, fn):
        k = self.phase
        self.ops.append(dict(eng="pool", fn=fn, r=(), w=(("PH", k), ("PH", k + 1)), dma=False))
        self.phase = k + 1

    def emit(self, final_tokens=()):
        nc = self.nc
        engs = ["pe", "act", "dve", "pool", "sp"]
        ncomp = {e: 0 for e in engs}
        ndma = {e: 0 for e in engs}
        for op in self.ops:
            if op["dma"]:
                ndma[op["eng"]] += 1
            else:
                ncomp[op["eng"]] += 1
        stack = contextlib.ExitStack()
        csems = {e: [stack.enter_context(nc.semaphore(f"c_{e}_{i}")) for i in range(ncomp[e] // EPOCH + 1)] for e in engs}
        dsems = {e: [stack.enter_context(nc.semaphore(f"d_{e}_{i}")) for i in range(NSLOT)] for e in engs if ndma[e]}
        last_w, readers = {}, {}
        ccount = {e: 0 for e in engs}
        dcount = {e: 0 for e in engs}
        slot_uses = {e: [0] * NSLOT for e in engs}
        plan = {e: [] for e in engs}
        run = {e: {} for e in engs}
        for op in self.ops:
            e = op["eng"]
            deps = []
            for t in op["r"]:
                ev = last_w.get(t)
                if ev is not None:
                    deps.append(ev)
            for t in op["w"]:
                ev = last_w.get(t)
                if ev is not None:
                    deps.append(ev)
                deps.extend(readers.get(t, ()))
            if op["dma"]:
                k = dcount[e]
                dcount[e] += 1
                slot = k % NSLOT
                sem = dsems[e][slot]
                prev = slot_uses[e][slot]
                if prev > 0:
                    deps.append((sem, 16 * prev, "dma"))
                slot_uses[e][slot] = prev + 1
                ev = (sem, 16 * (prev + 1), "dma")
                inc = (sem, 16)
            else:
                k = ccount[e]
                ccount[e] += 1
                sem = csems[e][k // EPOCH]
                ev = (sem, (k % EPOCH) + 1, e)
                inc = (sem, 1)
            need = {}
            for (s_, v_, src) in deps:
                if e == "pe" and src == "pe" and not op["dma"]:
                    continue
                key = id(s_)
                if v_ > need.get(key, (None, 0))[1]:
                    need[key] = (s_, v_)
            waits = []
            rn = run[e]
            for key, (s_, v_) in need.items():
                if rn.get(key, 0) >= v_:
                    continue
                rn[key] = v_
                waits.append((s_, v_))
            plan[e].append((op["fn"], waits, inc))
            for t in op["r"]:
                readers.setdefault(t, []).append(ev)
            for t in op["w"]:
                last_w[t] = ev
                readers[t] = []
        finals = []
        for t in final_tokens:
            ev = last_w.get(t)
            if ev is not None:
                finals.append((ev[0], ev[1]))
        self.stats = {e: (ccount[e], dcount[e]) for e in engs}
        with stack:
            with nc.Block() as block:
                def run_engine(e, eng, tail=None):
                    for (fn, waits, inc) in plan[e]:
                        for (s_, v_) in waits:
                            eng.wait_ge(s_, v_)
                        fn(eng).then_inc(inc[0], inc[1])
                    if tail:
                        for (s_, v_) in tail:
                            eng.wait_ge(s_, v_)

                @block.tensor
                def _(eng):
                    run_engine("pe", eng)

                @block.scalar
                def _(eng):
                    run_engine("act", eng)

                @block.vector
                def _(eng):
                    run_engine("dve", eng)

                @block.gpsimd
                def _(eng):
                    run_engine("pool", eng)

                @block.sync
                def _(eng):
                    run_engine("sp", eng, tail=finals)


class Arena:
    def __init__(self, nc, cap):
        self.nc, self.cap, self.n = nc, int(nc.sbuf_top) - 64, 0
        self.off = (int(nc.sbuf_base) + 127) // 128 * 128

    def alloc(self, name, shape, dt):
        sz = int(np.prod(shape[1:])) * (2 if dt == BF16 else 4)
        sz = (sz + 63) // 64 * 64
        t = self.nc.alloc_sbuf_tensor_at(f"{name}{self.n}", list(shape), dt, offset=self.off)
        self.n += 1
        self.off += sz
        assert self.off <= self.cap, (name, self.off, self.cap)
        return t


def build(stop=None, debug=False):
    DEBUG = debug
    nc = bass.Bass("TRN2", target_bir_lowering=False)
    S = Sched(nc)

    need = {"s1": ["ccol", "w_ada", "bada", "gmix", "gmlp", "gfin", "bglu"]}
    need["s2"] = need["s1"] + ["xw", "w_in", "mtok"]
    need["s3"] = need["s2"] + ["w_in", "mtok"]
    need["s4"] = need["s3"] + ["are2", "aim2", "ldt2", "X1", "X2", "CRI", "CIR", "Dm"]
    declared = []
    S.declared = declared

    def din(name, shape, dt=F32):
        if stop in need and name not in need[stop]:
            return None
        declared.append(name)
        return nc.dram_tensor(name, list(shape), dt, kind="ExternalInput").ap()

    d_xw = din("xw", [SEQ, D])
    d_ccol = din("ccol", [128, 16])
    d_wada = din("w_ada", [D, 6 * D])
    d_bada = din("bada", [128, 96])
    d_gmix = din("gmix", [128, 16])
    d_gmlp = din("gmlp", [128, 16])
    d_gfin = din("gfin", [128, 16])
    d_bglu = din("bglu", [128, 16])
    d_win = din("w_in", [D, 11312])
    d_wck1 = din("w_ck1", [32, 128, 128])
    d_wck2 = din("w_ck2", [128, 128])
    d_pek = din("pek", [128, 32])
    d_wcv1 = din("w_cv1", [32, 128, 128])
    d_wcv2 = din("w_cv2", [128, 128])
    d_pev = din("pev", [128, 32])
    d_are = din("are2", [128, 64])
    d_aim = din("aim2", [128, 64])
    d_ldt = din("ldt2", [128, 64])
    d_X1 = din("X1", [128, 64, 16])
    d_X2 = din("X2", [128, 64, 16])
    d_CRI = din("CRI", [128, 64, 32])
    d_CIR = din("CIR", [128, 64, 32])
    d_Dm = din("Dm", [32, 64, 32])
    d_wglu = din("w_glu", [D, D])
    d_wout = din("w_out", [D, D])
    d_wup = din("w_up", [D, 4 * D])
    d_wdn = din("w_down", [4 * D, D])
    d_ov = din("ov", [128, 2, 64])
    d_E = din("Esel", [64, 32, 128])
    d_tri = din("tri4", [128, 128])
    d_tris = din("tris4", [128, 128])
    d_mc4 = din("mc4", [128, 8, 128])
    d_mtok = din("mtok", [128, SEQ])
    d_cmpb = din("cmpb", [128, 2])
    d_winb = din("winb", [128, 12])
    d_scmul = din("scmul", [128, 8, 64])
    d_scadd = din("scadd", [128, 8, 64])
    d_scval = din("scval", [128, 8, 64])
    d_out = nc.dram_tensor("out", [NOWN, D], F32, kind="ExternalOutput").ap()
    uscr = nc.dram_tensor("uscr", [16, 128, SEQ], BF16, kind="Internal").ap()
    usscr = nc.dram_tensor("usscr", [D, SEQ], BF16, kind="Internal").ap()
    h1scr = nc.dram_tensor("h1scr", [16, 128, NOWN], F32, kind="Internal").ap()
    bsscr = nc.dram_tensor("bsscr", [16, 128, NOWN], BF16, kind="Internal").ap()
    kscr = nc.dram_tensor("kscr", [4, 3, 128, SEQ], BF16, kind="Internal").ap()
    kwscr = nc.dram_tensor("kwscr", [4, 128, 1536], BF16, kind="Internal").ap()
    vsscr = nc.dram_tensor("vsscr", [4, 128, 32, 128], BF16, kind="Internal").ap()
    vwscr = nc.dram_tensor("vwscr", [4, 128, 12, 128], BF16, kind="Internal").ap()
    qscr = nc.dram_tensor("qscr", [4, 128, 8, 4, 128], BF16, kind="Internal").ap()
    gscr = nc.dram_tensor("gscr", [4, 128, 8, 12], F32, kind="Internal").ap()
    dbg = {}
    if DEBUG:
        dbg["ada"] = nc.dram_tensor("dbg_ada", [128, 96], F32, kind="ExternalOutput").ap()
        dbg["uT"] = nc.dram_tensor("dbg_uT", [128, 16, 512], BF16, kind="ExternalOutput").ap()
        dbg["us"] = nc.dram_tensor("dbg_us", [128, 16, 512], BF16, kind="ExternalOutput").ap()
        dbg["yssm"] = nc.dram_tensor("dbg_yssm", [128, 8, D], BF16, kind="ExternalOutput").ap()
        dbg["yaT"] = nc.dram_tensor("dbg_yaT", [128, 16, NOWN], BF16, kind="ExternalOutput").ap()
    dbg_written = []

    def dump(name, ap, shape, dt, toks):
        if not DEBUG:
            return
        t_ = nc.dram_tensor("dbg_" + name, list(shape), dt, kind="ExternalOutput").ap()
        dma("sp", t_, ap, toks, ["dbg_" + name])
        dbg_written.append("dbg_" + name)

    def finish():
        S.emit(final_tokens=[f"out{q_}" for q_ in range(8)] + dbg_written)
        return nc, S

    d_win3 = d_win.rearrange("(k p) n -> p k n", p=128) if d_win is not None else None

    A = Arena(nc, int(nc.sbuf_bytes_remaining) - 256)
    ps = [nc.alloc_psum_tensor(f"ps{i}", [128, 512], F32) for i in range(7)]
    psb = nc.alloc_psum_tensor("psb", [128, 1024], BF16)
    psb2 = ps[6][:, :].bitcast(BF16)

    def mm(out, lhsT, rhs, start, stop, r, w):
        S.add("pe", lambda e: e.matmul(out, lhsT=lhsT, rhs=rhs, start=start, stop=stop), r, w)

    def tr(out, in_, ident, r, w):
        S.add("pe", lambda e: e.transpose(out=out, in_=in_, identity=ident), r, w)

    def actv(out, in_, func, r, w, bias=None, scale=None, accum=None):
        kw = {}
        if bias is not None:
            kw["bias"] = bias
        if scale is not None:
            kw["scale"] = scale
        if accum is not None:
            kw["accum_out"] = accum
        S.add("act", lambda e: e.activation(out=out, in_=in_, func=func, **kw), r, w)

    def tt(eng, out, in0, in1, op, r, w):
        S.add(eng, lambda e: e.tensor_tensor(out=out, in0=in0, in1=in1, op=op), r, w)

    def ts(eng, out, in0, s1, s2, op0, op1, r, w):
        if op1 is None:
            S.add(eng, lambda e: e.tensor_scalar(out=out, in0=in0, scalar1=s1, scalar2=None, op0=op0), r, w)
        else:
            S.add(eng, lambda e: e.tensor_scalar(out=out, in0=in0, scalar1=s1, scalar2=s2, op0=op0, op1=op1), r, w)

    def stt(eng, out, in0, sc, in1, op0, op1, r, w):
        S.add(eng, lambda e: e.scalar_tensor_tensor(out=out, in0=in0, scalar=sc, in1=in1, op0=op0, op1=op1), r, w)

    def cp(eng, out, in_, r, w):
        S.add(eng, lambda e: e.tensor_copy(out=out, in_=in_), r, w)

    def ms(eng, ap, val, w):
        S.add(eng, lambda e: e.memset(ap, val), (), w)

    def dma(q, out, in_, r, w):
        S.add(q, lambda e: e.dma_start(out=out, in_=in_), r, w, dma=True)

    def load(name, shape, src, dt=F32, q="sp"):
        t = A.alloc(name, shape, dt)
        dma(q, t[:], src, (), [name])
        return t

    identf = A.alloc("identf", [128, 128], F32)
    ident = A.alloc("ident", [128, 128], BF16)
    onesf = A.alloc("onesf", [128, 128], F32)
    one11 = A.alloc("one11", [1, 1], F32)
    sgn = A.alloc("sgn", [128, 1], F32)
    nsgn = A.alloc("nsgn", [128, 1], F32)
    bartile = A.alloc("bartile", [128, 1], F32)
    ms("pool", identf[:], 1.0, ["identf"])
    S.add("pool", lambda e: e.affine_select(out=identf[:], in_=identf[:], pattern=[[-1, 128]], compare_op=ALU.is_equal,
                                            fill=0.0, base=0, channel_multiplier=1), ["identf"], ["identf"])
    cp("dve", ident[:], identf[:], ["identf"], ["ident"])
    ms("dve", onesf[:], 1.0, ["onesf"])
    ms("dve", one11[:], 1.0, ["one11"])
    ms("dve", sgn[0:64, :], 1.0, ["sgn"])
    ms("dve", sgn[64:128, :], -1.0, ["sgn"])
    ms("dve", nsgn[0:64, :], -1.0, ["nsgn"])
    ms("dve", nsgn[64:128, :], 1.0, ["nsgn"])
    gmix = load("gmix", [128, 16], d_gmix)
    gmlp = load("gmlp", [128, 16], d_gmlp)
    gfin = load("gfin", [128, 16], d_gfin)
    bglu = load("bglu", [128, 16], d_bglu)
    bada = load("bada", [128, 96], d_bada)
    ccol = load("ccol", [128, 16], d_ccol)
    ada = A.alloc("ada", [128, 96], F32)
    m1s = A.alloc("m1s", [128, 16], F32)
    m2s = A.alloc("m2s", [128, 16], F32)
    persist = A.off

    def barrier():
        S.barrier(lambda e: e.memset(bartile[:], 0.0))

    cond = A.alloc("cond", [128, 16], F32)
    actv(cond[:], ccol[:], AF.Silu, ["ccol"], ["cond"])
    wab = [A.alloc("wab", [128, 16, 512], F32) for _ in range(2)]
    arow = [A.alloc("arow", [1, 512], F32) for _ in range(2)]
    d_wada3 = d_wada.rearrange("(k p) n -> p k n", p=128)
    for nt in range(24):
        wt = wab[nt % 2]
        tk = f"wab{nt % 2}"
        dma("sp", wt[:], d_wada3[:, :, nt * 512:(nt + 1) * 512], (), [tk])
        for k in range(16):
            mm(ps[0][0:1, :], cond[:, k:k + 1], wt[:, k, :], k == 0, k == 15, ["cond", tk], ["ps0"])
        ar = arow[nt % 2]
        ak = f"arow{nt % 2}"
        actv(ar[:], ps[0][0:1, :], AF.Copy, ["ps0"], [ak])
        for c4 in range(4):
            col = nt * 4 + c4
            mm(ps[6][:, col:col + 1], ar[0:1, c4 * 128:(c4 + 1) * 128], one11[0:1, 0:1], True, True, [ak, "one11"], ["ps6"])
    tt("dve", ada[:], ps[6][:, 0:96], bada[:], ALU.add, ["ps6", "bada"], ["ada"])
    tmp16 = A.alloc("tmp16", [128, 16], F32)
    ts("dve", tmp16[:], ada[:, 16:32], 1.0, None, ALU.add, None, ["ada"], ["tmp16"])
    tt("dve", m1s[:], tmp16[:], gmix[:], ALU.mult, ["tmp16", "gmix"], ["m1s"])
    ts("dve", tmp16[:], ada[:, 64:80], 1.0, None, ALU.add, None, ["ada"], ["tmp16"])
    tt("dve", m2s[:], tmp16[:], gmlp[:], ALU.mult, ["tmp16", "gmlp"], ["m2s"])
    if DEBUG:
        dma("sp", dbg["ada"], ada[:], ["ada"], ["dbg_ada"])
        dbg_written.append("dbg_ada")
    if stop == "s1":
        return finish()
    barrier()
    A.off = persist

    xtb = [A.alloc("xt", [128, 4, D], F32) for _ in range(2)]
    xn = A.alloc("xn", [128, 4, D], BF16)
    junk = A.alloc("junk", [128, D], BF16)
    uTb = [A.alloc("uT", [128, 16, 512], BF16) for _ in range(2)]
    ssq = A.alloc("ssq", [128, 32], F32)
    rstd = A.alloc("rstd", [128, 32], F32)
    ms("dve", ssq[:], 0.0, ["ssq"])
    uscr3 = uscr.rearrange("c p t -> p c t")
    wssm = A.alloc("wssm", [128, 16, D], BF16)
    for i in range(4):
        dma("pool", wssm[:, :, i * 512:(i + 1) * 512], d_win3[:, :, 5168 + i * 512:5168 + (i + 1) * 512], (), ["wssm"])
    mtok = A.alloc("mtok", [128, SEQ], BF16)
    dma("pool", mtok[:], d_mtok, (), ["mtok"])
    usT = A.alloc("usT", [128, 16, 512], BF16)
    usscr3 = usscr.rearrange("(c p) t -> p c t", p=128)
    import os
    for T in range(int(os.environ.get("KLIM", "8"))):
        xt = xtb[T % 2]
        xk = f"xt{T % 2}"
        dma("sp", xt[:], d_xw[T * 512:(T + 1) * 512, :].rearrange("(a p) f -> p a f", p=128), (), [xk])
        KS = os.environ.get("KSKIP", "")
        if "norm" in KS:
            ms("dve", xn[:], 1.0, ["xn"])
        for a in range(4 if "norm" not in KS else 0):
            actv(junk[:], xt[:, a, :], AF.Square, [xk, "ssq"], ["junk", "ssq"], accum=ssq[:, T * 4 + a:T * 4 + a + 1])
        if "norm" not in KS:
            ts("dve", rstd[:, T * 4:T * 4 + 4], ssq[:, T * 4:T * 4 + 4], 1.0 / D, EPS, ALU.mult, ALU.add, ["ssq"], ["rstd"])
            actv(rstd[:, T * 4:T * 4 + 4], rstd[:, T * 4:T * 4 + 4], AF.Sqrt, ["rstd"], ["rstd"])
            S.add("dve", (lambda o, i: (lambda e: e.reciprocal(out=o, in_=i)))(rstd[:, T * 4:T * 4 + 4], rstd[:, T * 4:T * 4 + 4]), ["rstd"], ["rstd"])
        for a in range(4 if "norm" not in KS else 0):
            actv(xn[:, a, :], xt[:, a, :], AF.Copy, [xk, "rstd"], ["xn"], scale=rstd[:, T * 4 + a:T * 4 + a + 1])
        uT = uTb[T % 2]
        uk = f"uT{T % 2}"
        if "tr" in KS:
            ms("dve", uT[:], 2.0, [uk])
        for f in range(16 if "tr" not in KS else 0):
            hk = "psb" if f % 2 == 0 else "ps6"
            pv = psb[:, 0:512] if f % 2 == 0 else psb2[:, 0:512]
            for a in range(4):
                tr(pv[:, a * 128:(a + 1) * 128], xn[:, a, f * 128:(f + 1) * 128], ident[:], ["xn", "ident"], [hk])
            actv(uT[:, f, :], pv, AF.Identity, [hk, "m1s", "ada"], [uk], bias=ada[:, f:f + 1], scale=m1s[:, f:f + 1])
        dma("sp", uscr3[:, :, T * 512:(T + 1) * 512], uT[:], [uk], ["uscr"])
        if DEBUG and T == int(os.environ.get("KLIM", "8")) - 1:
            dma("sp", dbg["uT"], uT[:], [uk], ["dbg_uT"])
            dbg_written.append("dbg_uT")
        for cc in range(16):
            pk = f"ps{cc % 2}"
            for k in range(16):
                mm(ps[cc % 2][:, :], wssm[:, k, cc * 128:(cc + 1) * 128], uT[:, k, :], k == 0, k == 15, ["wssm", uk], [pk])
            tt("dve", usT[:, cc, :], ps[cc % 2][:, :], mtok[:, T * 512:(T + 1) * 512], ALU.mult, [pk, "mtok"], ["usT"])
        dma("sp", usscr3[:, :, T * 512:(T + 1) * 512], usT[:], ["usT"], ["usscr"])
        if DEBUG and T == 7:
            dma("sp", dbg["us"], usT[:], ["usT"], ["dbg_us"])
            dbg_written.append("dbg_us")
    if stop in ("s2", "s3"):
        return finish()
    barrier()
    A.off = persist

    Y = A.alloc("Y", [128, 8, D], BF16)
    ssm_base = A.off
    TBb = [A.alloc("TB", [128, 2, NOWN], F32)] * 2
    wre = A.alloc("wre", [128, NOWN], F32)
    wim = A.alloc("wim", [128, NOWN], F32)
    zbuf = A.alloc("zbuf", [128, NOWN], F32)
    E_all = A.alloc("E_all", [128, 16, 2, 128], F32)
    Bs_all = A.alloc("Bs_all", [128, 16, 2, 128], F32)
    Bs_bf = A.alloc("Bs_bf", [128, 16, 2, 128], BF16)
    A_all = A.alloc("A_all", [128, 16, 2, 24], F32)
    csE = A.alloc("csE", [128, 16, 2], F32)
    csE2 = A.alloc("csE2", [128, 16, 2], F32)
    csE4 = A.alloc("csE4", [128, 16, 2], F32)
    cmb = [A.alloc("cm", [128, 16, 2], F32) for _ in range(2)]
    bt1 = wre[:].rearrange("p (g n) -> p g n", n=64)
    bt2 = wim[:].rearrange("p (g n) -> p g n", n=64)
    sq1 = A.alloc("sq1", [128, 16], F32)
    sq2 = A.alloc("sq2", [128, 16], F32)
    utokb = [A.alloc("utok", [128, 24, 32], BF16) for _ in range(2)]
    BtTb = [A.alloc("BtT", [128, 2, 128], BF16) for _ in range(2)]
    _bsflat = Bs_all[:].rearrange("p g a n -> p (g a n)")
    P1, P2, P3, P4 = [_bsflat[:, k_ * 768:(k_ + 1) * 768].rearrange("p (c i) -> p c i", i=32) for k_ in range(4)]
    Xc = A.alloc("Xc", [128, 2, 32], F32)
    q1t = A.alloc("q1t", [128, 32], F32)
    q2t = A.alloc("q2t", [128, 32], F32)
    xs = A.alloc("xs", [128, 4], F32)
    zi = A.alloc("zi", [128, 2], F32)
    t1 = A.alloc("t1", [128, 512], F32)
    t2 = A.alloc("t2", [128, 512], F32)
    ptA, ptB = t1, t2
    wAp = A.alloc("wAp", [128, 16, 908], BF16)
    uTp = A.alloc("uTp", [128, 16, 512], BF16)
    stg = [A.alloc("stg", [128, 512], BF16) for _ in range(4)]
    stgg = A.alloc("stgg", [128, 4, 12], F32)
    csb = [A.alloc("cs", [128, 2], F32) for _ in range(2)]
    cst = A.alloc("cst", [128, 2], F32)
    Zcr = A.alloc("Zcr", [128, NOWN], BF16)
    Zsr = A.alloc("Zsr", [128, NOWN], BF16)
    Zci = A.alloc("Zci", [128, NOWN], BF16)
    Zsi = A.alloc("Zsi", [128, NOWN], BF16)
    upb = [A.alloc("up", [32, SEQ], BF16) for _ in range(2)]
    BTre = A.alloc("BTre", [32, 16, 128], BF16)
    BTim = A.alloc("BTim", [32, 16, 128], BF16)
    Ha = A.alloc("Ha", [128, 16, 32], BF16)
    Hni = A.alloc("Hni", [128, 16, 32], BF16)
    Hnr = A.alloc("Hnr", [128, 16, 32], BF16)
    Dmb = A.alloc("Dmb", [32, 16, 32], BF16)
    Mre = A.alloc("Mre", [128, 16, 32], BF16)
    Mim = A.alloc("Mim", [128, 16, 32], BF16)
    blk_base = A.off

    def gen_tables(pair, pl, sm, tk):
        TB = TBb[0]
        TBk = "TB0"
        cp("dve", TB[:, :, 0:128], E_all[:, pl, :, :], ["E_all"], [TBk])
        for (m, ct, ck) in [(128, csE, "csE"), (256, csE2, "csE2"), (512, csE4, "csE4")]:
            c_ap, s_ap = ct[:, pl, 0:1], ct[:, pl, 1:2]
            Co, So = TB[:, 0, 0:m], TB[:, 1, 0:m]
            ts("dve", ptA[:, 0:m], So, s_ap, None, ALU.mult, None, [TBk, ck], ["t1"])
            stt("dve", TB[:, 0, m:2 * m], Co, c_ap, ptA[:, 0:m], ALU.mult, ALU.subtract, [TBk, ck, "t1"], [TBk])
            ts("dve", ptB[:, 0:m], So, c_ap, None, ALU.mult, None, [TBk, ck], ["t2"])
            stt("dve", TB[:, 1, m:2 * m], Co, s_ap, ptB[:, 0:m], ALU.mult, ALU.add, [TBk, ck, "t2"], [TBk])

    def sq_mult(dst, dk, src, sk):
        tt("dve", sq1[:], src[:, :, 1], src[:, :, 1], ALU.mult, [sk], ["sq1"])
        tt("dve", sq2[:], src[:, :, 0], src[:, :, 0], ALU.mult, [sk], ["sq2"])
        tt("dve", dst[:, :, 0], sq2[:], sq1[:], ALU.subtract, ["sq1", "sq2"], [dk])
        tt("dve", sq1[:], src[:, :, 0], src[:, :, 1], ALU.mult, [sk], ["sq1"])
        tt("dve", dst[:, :, 1], sq1[:], sq1[:], ALU.add, ["sq1"], [dk])

    def block_table(tab, tname, L, m_re0, m_im0, mkeys, left, extra_w=[]):
        idx0 = L - 1 if left else 0
        ms("dve", tab[:, :, 0, idx0:idx0 + 1], 1.0, [tname] + extra_w)
        ms("dve", tab[:, :, 1, idx0:idx0 + 1], 0.0, [tname] + extra_w)
        cp("dve", cmb[0][:, :, 0], m_re0, mkeys, ["cm0"])
        cp("dve", cmb[0][:, :, 1], m_im0, mkeys, ["cm0"])
        m, k = 1, 0
        while m < L:
            cur = cmb[k % 2]
            ck = f"cm{k % 2}"
            if left:
                d0, d1 = max(0, L - 2 * m), L - m
                s0 = d0 + m
            else:
                d0, d1 = m, min(L, 2 * m)
                s0 = 0
            n = d1 - d0
            mre = cur[:, :, 0:1].to_broadcast([128, 16, n])
            mim = cur[:, :, 1:2].to_broadcast([128, 16, n])
            sre, sim = tab[:, :, 0, s0:s0 + n], tab[:, :, 1, s0:s0 + n]
            tt("dve", bt1[:, :, 0:n], sre, mre, ALU.mult, [tname, ck], ["wre"])
            tt("dve", bt2[:, :, 0:n], sim, mim, ALU.mult, [tname, ck], ["wim"])
            tt("dve", tab[:, :, 0, d0:d1], bt1[:, :, 0:n], bt2[:, :, 0:n], ALU.subtract, ["wre", "wim"], [tname] + extra_w)
            tt("dve", bt1[:, :, 0:n], sre, mim, ALU.mult, [tname, ck], ["wre"])
            tt("dve", bt2[:, :, 0:n], sim, mre, ALU.mult, [tname, ck], ["wim"])
            tt("dve", tab[:, :, 1, d0:d1], bt1[:, :, 0:n], bt2[:, :, 0:n], ALU.add, ["wre", "wim"], [tname] + extra_w)
            m *= 2
            nxt = cmb[(k + 1) % 2]
            nk = f"cm{(k + 1) % 2}"
            tt("dve", sq1[:], cur[:, :, 1], cur[:, :, 1], ALU.mult, [ck], ["sq1"])
            tt("dve", sq2[:], cur[:, :, 0], cur[:, :, 0], ALU.mult, [ck], ["sq2"])
            tt("dve", nxt[:, :, 0], sq2[:], sq1[:], ALU.subtract, ["sq1", "sq2"], [nk])
            tt("dve", sq1[:], cur[:, :, 0], cur[:, :, 1], ALU.mult, [ck], ["sq1"])
            tt("dve", nxt[:, :, 1], sq1[:], sq1[:], ALU.add, ["sq1"], [nk])
            k += 1
        return cmb[k % 2], f"cm{k % 2}"

    def interleave(a0, a1, a2):
        la, lb = S.ops[a0:a1], S.ops[a1:a2]
        out, i, j = [], 0, 0
        while i < len(la) or j < len(lb):
            if j >= len(lb) or (i < len(la) and i * len(lb) <= j * len(la)):
                out.append(la[i]); i += 1
            else:
                out.append(lb[j]); j += 1
        S.ops[a0:a2] = out

    SCALE = 128.0 ** -0.5
    kvbase = [2048, 2560, 3072, 3584, 4096, 4608]

    def gen_proj(g):
        cnt = [0, 0]

        def nxt_ps():
            i_ = 5 + cnt[0] % 2
            cnt[0] += 1
            return ps[i_], f"ps{i_}"

        def nxt_stg():
            i_ = cnt[1] % 4
            cnt[1] += 1
            return stg[i_], f"stg{i_}"

        marks = []

        def mark():
            marks.append(len(S.ops))

        def fmaj(col0, dst, T, scale=None, view=None):
            mark()
            p_, pk = nxt_ps()
            for k in range(16):
                mm(p_[:, :], wAp[:, k, col0:col0 + 128], uTp[:, k, :], k == 0, k == 15, ["wAp", "uTp"], [pk])
            st_, sk = nxt_stg()
            actv(st_[:], p_[:, :], AF.Copy, [pk], [sk], scale=scale)
            dma("sp", dst, st_[:] if view is None else st_[:].rearrange(view, t=128), [sk], [f"kscr{g}"])

        def tmaj(col0, dsts, T):
            st_, sk = nxt_stg()
            for a in range(4):
                mark()
                p_, pk = nxt_ps()
                for k in range(16):
                    mm(p_[:, 0:128], uTp[:, k, a * 128:(a + 1) * 128], wAp[:, k, col0:col0 + 128], k == 0, k == 15, ["wAp", "uTp"], [pk])
                actv(st_[:, a * 128:(a + 1) * 128], p_[:, 0:128], AF.Copy, [pk], [sk])
            dma("sp", dsts, st_[:].rearrange("p (a t) -> p a t", t=128), [sk], [f"kscr{g}"])

        start = len(S.ops)
        dma("pool", wAp[:, :, 0:512], d_win3[:, :, 512 * g:512 * g + 512], (), ["wAp"])
        dma("pool", wAp[:, :, 512:524], d_win3[:, :, 5120 + 12 * g:5120 + 12 * g + 12], (), ["wAp"])
        for i in range(3):
            dma("pool", wAp[:, :, 524 + 128 * i:524 + 128 * i + 128], d_win3[:, :, kvbase[i] + 128 * g:kvbase[i] + 128 * g + 128], (), ["wAp"])
        for T in range(8):
            dma("sp", uTp[:], uscr3[:, :, T * 512:(T + 1) * 512], ["uscr"], ["uTp"])
            for i in range(3):
                fmaj(524 + 128 * i, kscr[g, i, :, T * 512:(T + 1) * 512], T)
            if T >= 6:
                qi0 = (T - 6) * 4
                for h in range(4):
                    fmaj(h * 128, qscr[g, :, qi0:qi0 + 4, h, :], T, scale=SCALE, view="p (a t) -> p a t")
                mark()
                for a in range(4):
                    p_, pk = nxt_ps()
                    for k in range(16):
                        mm(p_[:, 0:12], uTp[:, k, a * 128:(a + 1) * 128], wAp[:, k, 512:524], k == 0, k == 15, ["wAp", "uTp"], [pk])
                    actv(stgg[:, a, :], p_[:, 0:12], AF.Sigmoid, [pk], ["stgg"])
                dma("sp", gscr[g, :, qi0:qi0 + 4, :], stgg[:], ["stgg"], [f"kscr{g}"])
        for i in range(3):
            dma("pool", wAp[:, :, 128 * i:128 * i + 128], d_win3[:, :, kvbase[3 + i] + 128 * g:kvbase[3 + i] + 128 * g + 128], (), ["wAp"])
        for T in range(8):
            dma("sp", uTp[:], uscr3[:, :, T * 512:(T + 1) * 512], ["uscr"], ["uTp"])
            tmaj(0, vsscr[g, :, T * 4:T * 4 + 4, :], T)
            if T >= 5:
                fmaj(128, kwscr[g, :, (T - 5) * 512:(T - 5) * 512 + 512], T)
                tmaj(256, vwscr[g, :, (T - 5) * 4:(T - 5) * 4 + 4, :], T)
        ops = S.ops[start:]
        del S.ops[start:]
        bounds = [0] + [m_ - start for m_ in marks[1:]] + [len(ops)]
        return [ops[bounds[i_]:bounds[i_ + 1]] for i_ in range(len(bounds) - 1)]

    pq = []

    def proj_slot(n=1):
        for _ in range(n):
            if pq:
                S.ops.extend(pq.pop(0))

    for B in range(4):
        A.off = blk_base
        pq.extend(gen_proj(B))
        per_pair = (len(pq) + 15) // 16
        gs = slice(16 * B, 16 * B + 16)
        sfx = f"_{B}"
        are = load("are" + sfx, [128, 16], d_are[:, gs])
        aim = load("aim" + sfx, [128, 16], d_aim[:, gs])
        ldt = load("ldt" + sfx, [128, 16], d_ldt[:, gs])
        XR = load("XR" + sfx, [128, 16, 16], d_X1[:, gs, :])
        XI = load("XI" + sfx, [128, 16, 16], d_X2[:, gs, :])
        HCR = load("HCR" + sfx, [128, 16, 32], d_CRI[:, gs, :])
        HCI = load("HCI" + sfx, [128, 16, 32], d_CIR[:, gs, :])
        Dmf = load("Dmf" + sfx, [32, 16, 32], d_Dm[:, gs, :])
        sm = {}
        for nm in ["dt", "adt", "mag", "ang", "q1", "q2", "angs", "angc", "s1", "c1", "lre", "lim", "nr", "den", "cre", "cim", "u1", "u2"]:
            sm[nm] = A.alloc(nm + sfx, [128, 16], F32)
        big1 = A.alloc("big1" + sfx, [128, 16, 16], F32)
        big2 = A.alloc("big2" + sfx, [128, 16, 16], F32)
        SB = A.alloc("SB" + sfx, [128, 16, 16], F32)
        tk = lambda n: n + sfx

        def el(eng, o, a, b, op):
            tt(eng, sm[o][:], sm[a][:], sm[b][:], op, [tk(a), tk(b)], [tk(o)])

        actv(sm["dt"][:], ldt[:], AF.Exp, ["ldt" + sfx], [tk("dt")])
        tt("dve", sm["adt"][:], are[:], sm["dt"][:], ALU.mult, ["are" + sfx, tk("dt")], [tk("adt")])
        actv(sm["mag"][:], sm["adt"][:], AF.Exp, [tk("adt")], [tk("mag")])
        tt("dve", sm["ang"][:], aim[:], sm["dt"][:], ALU.mult, ["aim" + sfx, tk("dt")], [tk("ang")])

        def reduce_angle(src, dst):
            ts("dve", sm["q1"][:], sm[src][:], PI, None, ALU.is_gt, None, [tk(src)], [tk("q1")])
            for th in (3 * PI, 5 * PI, 7 * PI):
                ts("dve", sm["q2"][:], sm[src][:], th, None, ALU.is_gt, None, [tk(src)], [tk("q2")])
                el("dve", "q1", "q1", "q2", ALU.add)
            stt("dve", sm[dst][:], sm["q1"][:], -2 * PI, sm[src][:], ALU.mult, ALU.add, [tk("q1"), tk(src)], [tk(dst)])

        reduce_angle("ang", "angs")
        actv(sm["s1"][:], sm["angs"][:], AF.Sin, [tk("angs")], [tk("s1")])
        ts("dve", sm["angc"][:], sm["ang"][:], PI / 2, None, ALU.add, None, [tk("ang")], [tk("angc")])
        reduce_angle("angc", "angs")
        actv(sm["c1"][:], sm["angs"][:], AF.Sin, [tk("angs")], [tk("c1")])
        el("dve", "lre", "mag", "c1", ALU.mult)
        el("dve", "lim", "mag", "s1", ALU.mult)
        ts("dve", sm["nr"][:], sm["lre"][:], -1.0, None, ALU.add, None, [tk("lre")], [tk("nr")])
        tt("dve", sm["den"][:], are[:], are[:], ALU.mult, ["are" + sfx], [tk("den")])
        tt("dve", sm["u1"][:], aim[:], aim[:], ALU.mult, ["aim" + sfx], [tk("u1")])
        el("dve", "den", "den", "u1", ALU.add)
        S.add("dve", (lambda o: (lambda e: e.reciprocal(out=o, in_=o)))(sm["den"][:]), [tk("den")], [tk("den")])
        tt("dve", sm["u1"][:], sm["nr"][:], are[:], ALU.mult, [tk("nr"), "are" + sfx], [tk("u1")])
        tt("dve", sm["u2"][:], sm["lim"][:], aim[:], ALU.mult, [tk("lim"), "aim" + sfx], [tk("u2")])
        el("dve", "u1", "u1", "u2", ALU.add)
        el("dve", "cre", "u1", "den", ALU.mult)
        tt("dve", sm["u1"][:], sm["lim"][:], are[:], ALU.mult, [tk("lim"), "are" + sfx], [tk("u1")])
        tt("dve", sm["u2"][:], sm["nr"][:], aim[:], ALU.mult, [tk("nr"), "aim" + sfx], [tk("u2")])
        el("dve", "u1", "u1", "u2", ALU.subtract)
        el("dve", "cim", "u1", "den", ALU.mult)

        def bc(n):
            return sm[n][:].unsqueeze(2).to_broadcast([128, 16, 16])

        def build_M(ka, xa, kb, xb, op, Mdst, mname):
            tt("dve", big1[:], xa[:], bc(ka), ALU.mult, [tk(ka), "XR" + sfx, "XI" + sfx], ["big1" + sfx])
            tt("dve", big2[:], xb[:], bc(kb), ALU.mult, [tk(kb), "XR" + sfx, "XI" + sfx], ["big2" + sfx])
            tt("dve", SB[:], big1[:], big2[:], op, ["big1" + sfx, "big2" + sfx], ["SB" + sfx])
            ms("dve", Mdst[:], 0.0, [mname])
            cp("dve", Mdst[0:64, :, 0:16], SB[0:64, :, :], ["SB" + sfx], [mname])
            cp("dve", Mdst[64:128, :, 16:32], SB[64:128, :, :], ["SB" + sfx], [mname])

        build_M("cre", XR, "cim", XI, ALU.subtract, Mre, "Mre")
        for g0 in range(0, 16, 8):
            for g in range(g0, g0 + 8):
                tr(psb[0:32, (g % 8) * 128:(g % 8) * 128 + 128], Mre[:, g, :], ident[:], ["Mre", "ident"], ["psb"])
            actv(BTre[:, g0:g0 + 8, :], psb[0:32, :].rearrange("p (g s) -> p g s", s=128), AF.Copy, ["psb"], ["BTre"])
        build_M("cre", XI, "cim", XR, ALU.add, Mim, "Mim")
        for g0 in range(0, 16, 8):
            for g in range(g0, g0 + 8):
                tr(psb[0:32, (g % 8) * 128:(g % 8) * 128 + 128], Mim[:, g, :], ident[:], ["Mim", "ident"], ["psb"])
            actv(BTim[:, g0:g0 + 8, :], psb[0:32, :].rearrange("p (g s) -> p g s", s=128), AF.Copy, ["psb"], ["BTim"])
        cp("dve", Ha[:], HCR[:], ["HCR" + sfx], ["Ha"])
        ts("dve", Hni[:], HCI[:], -1.0, None, ALU.mult, None, ["HCI" + sfx], ["Hni"])
        ts("dve", Hnr[:], HCR[:], -1.0, None, ALU.mult, None, ["HCR" + sfx], ["Hnr"])
        cp("dve", Dmb[:], Dmf[:], ["Dmf" + sfx], ["Dmb"])
        fin, fk = block_table(E_all, "E_all", 128, sm["c1"][:], sm["s1"][:], [tk("c1"), tk("s1")], False)
        cp("dve", csE[:], fin[:], [fk], ["csE"])
        sq_mult(csE2, "csE2", csE, "csE")
        sq_mult(csE4, "csE4", csE2, "csE2")
        fin, fk = block_table(Bs_all, "Bs_all", 128, sm["lre"][:], sm["lim"][:], [tk("lre"), tk("lim")], True, extra_w=["P1", "P2", "P3", "P4"])
        cp("dve", Bs_bf[:], Bs_all[:], ["Bs_all", "P1", "P2", "P3", "P4"], ["Bs_bf"])
        cp("dve", sm["u1"][:], fin[:, :, 0], [fk], [tk("u1")])
        cp("dve", sm["u2"][:], fin[:, :, 1], [fk], [tk("u2")])
        block_table(A_all, "A_all", 24, sm["u1"][:], sm["u2"][:], [tk("u1"), tk("u2")], True)

        def front(pair_, pl_):
            up_ = upb[pair_ % 2]
            upk_ = f"up{pair_ % 2}"
            ut_, utk_ = utokb[pair_ % 2], f"utok{pair_ % 2}"
            bt_, btk_ = BtTb[pair_ % 2], f"BtT{pair_ % 2}"
            dma("sp", up_[:], usscr[32 * pair_:32 * pair_ + 32, :], ["usscr"], [upk_])
            for c in range(24):
                tr(psb[:, c * 32:(c + 1) * 32], up_[0:32, c * 128:(c + 1) * 128], ident[0:32, 0:32], [upk_, "ident"], ["psb"])
            tr(psb[:, 768:896], Bs_bf[:, pl_, 0, :], ident[:], ["Bs_bf", "ident"], ["psb"])
            tr(psb[:, 896:1024], Bs_bf[:, pl_, 1, :], ident[:], ["Bs_bf", "ident"], ["psb"])
            actv(ut_[:].rearrange("p c i -> p (c i)"), psb[:, 0:768], AF.Copy, ["psb"], [utk_])
            actv(bt_[:].rearrange("p a i -> p (a i)"), psb[:, 768:1024], AF.Copy, ["psb"], [btk_])
            uf_ = ut_[:].rearrange("p c i -> p (c i)")
            for a_ in range(2):
                mm(ps[2 * a_][:, 0:512], bt_[:, a_, :], uf_[:, 0:512], True, True, [btk_, utk_], [f"ps{2 * a_}"])
                mm(ps[2 * a_ + 1][:, 0:256], bt_[:, a_, :], uf_[:, 512:768], True, True, [btk_, utk_], [f"ps{2 * a_ + 1}"])

        front(16 * B, 0)
        for pl in range(16):
            pair = 16 * B + pl
            up = upb[pair % 2]
            upk = f"up{pair % 2}"
            TB = TBb[0]
            TBk = "TB0"
            gen_tables(pair, pl, sm, tk)
            slots = [per_pair // 5 + (1 if (i_ + 1) % 5 >= 5 - per_pair % 5 else 0) for i_ in range(5)]
            Are, Aim = A_all[:, pl, 0, :], A_all[:, pl, 1, :]

            def abc(a, c0, c1_):
                return a[:, c0:c1_].unsqueeze(2).to_broadcast([128, c1_ - c0, 32])

            proj_slot(slots[0])
            for (Pd, pdk, glo, ghi, gk0, gk1, av) in [(P1, "P1", ps[0], ps[1], "ps0", "ps1", Are), (P4, "P4", ps[0], ps[1], "ps0", "ps1", Aim),
                                                       (P2, "P2", ps[2], ps[3], "ps2", "ps3", Aim), (P3, "P3", ps[2], ps[3], "ps2", "ps3", Are)]:
                tt("dve", Pd[:, 0:16, :], glo[:, 0:512].rearrange("p (c i) -> p c i", i=32), abc(av, 0, 16), ALU.mult, [gk0, "A_all"], [pdk])
                tt("dve", Pd[:, 16:24, :], ghi[:, 0:256].rearrange("p (c i) -> p c i", i=32), abc(av, 16, 24), ALU.mult, [gk1, "A_all"], [pdk])
            tt("dve", P1[:], P1[:], P2[:], ALU.subtract, ["P1", "P2"], ["P1"])
            tt("dve", P3[:], P3[:], P4[:], ALU.add, ["P3", "P4"], ["P3"])
            S.add("dve", (lambda o, i: (lambda e: e.tensor_reduce(out=o, in_=i, axis=AX.X, op=ALU.add)))(Xc[:, 0, :], P1[:].rearrange("p c i -> p i c")), ["P1"], ["Xc"])
            S.add("dve", (lambda o, i: (lambda e: e.tensor_reduce(out=o, in_=i, axis=AX.X, op=ALU.add)))(Xc[:, 1, :], P3[:].rearrange("p c i -> p i c")), ["P3"], ["Xc"])
            tt("dve", q1t[:], Mre[:, pl, :], Xc[:, 0, :], ALU.mult, ["Mre", "Xc"], ["q1t"])
            tt("dve", q2t[:], Mim[:, pl, :], Xc[:, 1, :], ALU.mult, ["Mim", "Xc"], ["q2t"])
            tt("dve", q1t[:], q1t[:], q2t[:], ALU.subtract, ["q1t", "q2t"], ["q1t"])
            S.add("dve", (lambda o, i: (lambda e: e.tensor_reduce(out=o, in_=i, axis=AX.X, op=ALU.add)))(xs[:, 0:1], q1t[:]), ["q1t"], ["xs"])
            tt("dve", q1t[:], Mre[:, pl, :], Xc[:, 1, :], ALU.mult, ["Mre", "Xc"], ["q1t"])
            tt("dve", q2t[:], Mim[:, pl, :], Xc[:, 0, :], ALU.mult, ["Mim", "Xc"], ["q2t"])
            tt("dve", q1t[:], q1t[:], q2t[:], ALU.add, ["q1t", "q2t"], ["q1t"])
            S.add("dve", (lambda o, i: (lambda e: e.tensor_reduce(out=o, in_=i, axis=AX.X, op=ALU.add)))(xs[:, 1:2], q1t[:]), ["q1t"], ["xs"])
            c1p, s1p = sm["c1"][:, pl:pl + 1], sm["s1"][:, pl:pl + 1]
            ts("dve", xs[:, 2:3], xs[:, 1:2], s1p, None, ALU.mult, None, ["xs", tk("s1")], ["xs2"])
            stt("dve", zi[:, 0:1], xs[:, 0:1], c1p, xs[:, 2:3], ALU.mult, ALU.subtract, ["xs", "xs2", tk("c1")], ["zi"])
            ts("dve", xs[:, 3:4], xs[:, 0:1], s1p, None, ALU.mult, None, ["xs", tk("s1")], ["xs3"])
            stt("dve", zi[:, 1:2], xs[:, 1:2], c1p, xs[:, 3:4], ALU.mult, ALU.add, ["xs", "xs3", tk("c1")], ["zi"])
            for T in range(2):
                sl = slice(T * 512, (T + 1) * 512)
                gsl = slice(SEQ - NOWN + T * 512, SEQ - NOWN + (T + 1) * 512)
                pa, pb_ = ps[2 * T], ps[2 * T + 1]
                ka, kb = f"ps{2 * T}", f"ps{2 * T + 1}"
                mm(pa[:, :], BTre[0:32, pl, :], up[0:32, gsl], True, True, ["BTre", upk], [ka])
                mm(pb_[:, :], BTim[0:32, pl, :], up[0:32, gsl], True, True, ["BTim", upk], [kb])
                proj_slot(slots[1 + T])
                if T == 1 and pl < 15:
                    pass
                tt("dve", t1[:], pa[:, :], TB[:, 0, sl], ALU.mult, [ka, TBk], ["t1"])
                tt("dve", t2[:], pb_[:, :], TB[:, 1, sl], ALU.mult, [kb, TBk], ["t2"])
                tt("dve", wre[:, sl], t1[:], t2[:], ALU.add, ["t1", "t2"], ["wre"])
                tt("dve", t1[:], pb_[:, :], TB[:, 0, sl], ALU.mult, [kb, TBk], ["t1"])
                tt("dve", t2[:], pa[:, :], TB[:, 1, sl], ALU.mult, [ka, TBk], ["t2"])
                tt("dve", wim[:, sl], t1[:], t2[:], ALU.subtract, ["t1", "t2"], ["wim"])
            for (wsrc, wkey, Zc_, zck, Zs_, zsk, zcol) in [(wre, "wre", Zcr, "Zcr", Zsr, "Zsr", 0), (wim, "wim", Zci, "Zci", Zsi, "Zsi", 1)]:
                S.add("dve", (lambda zz, mg, ww, ini: (lambda e: e.tensor_tensor_scan(out=zz, data0=mg, data1=ww, initial=ini, op0=ALU.mult, op1=ALU.add)))(
                    zbuf[:], sm["mag"][:, pl:pl + 1].to_broadcast([128, NOWN]), wsrc[:], zi[:, zcol:zcol + 1]), [tk("mag"), wkey, "zi"], ["zbuf"])
                tt("dve", Zc_[:], TB[:, 0, :], zbuf[:], ALU.mult, [TBk, "zbuf"], [zck])
                tt("dve", Zs_[:], TB[:, 1, :], zbuf[:], ALU.mult, [TBk, "zbuf"], [zsk])
            if pl < 15:
                front(pair + 1, pl + 1)
                proj_slot(slots[3])
            for q in range(8):
                o = ps[4][:, q * 32:(q + 1) * 32]
                qs_ = slice(q * 128, (q + 1) * 128)
                mm(o, Zcr[:, qs_], Ha[:, pl, :], True, False, ["Zcr", "Ha"], ["ps4"])
                mm(o, Zci[:, qs_], Hni[:, pl, :], False, False, ["Zci", "Hni"], ["ps4"])
                mm(o, Zsr[:, qs_], Hni[:, pl, :], False, False, ["Zsr", "Hni"], ["ps4"])
                mm(o, Zsi[:, qs_], Hnr[:, pl, :], False, False, ["Zsi", "Hnr"], ["ps4"])
                mm(o, up[0:32, SEQ - NOWN + q * 128:SEQ - NOWN + (q + 1) * 128], Dmb[0:32, pl, :], False, True, [upk, "Dmb"], ["ps4"])
            proj_slot(slots[4])
            actv(Y[:, :, 32 * pair:32 * pair + 32], ps[4][:, 0:256].rearrange("p (q c) -> p q c", c=32), AF.Copy, ["ps4"], ["Y"])
        proj_slot(len(pq))
    if DEBUG:
        dma("sp", dbg["yssm"], Y[:], ["Y"], ["dbg_yssm"])
        dbg_written.append("dbg_yssm")
    if stop == "s4":
        return finish()
    barrier()
    A.off = ssm_base
    BsT = A.alloc("BsT", [128, 16, NOWN], BF16)
    g1 = A.alloc("g1", [128, D], F32)
    for q in range(8):
        tt("dve", g1[:], Y[:, q, :], Y[:, q, :], ALU.mult, ["Y"], ["g1"])
        ts("dve", g1[:], g1[:], 0.044715, 1.0, ALU.mult, ALU.add, ["g1"], ["g1"])
        tt("dve", g1[:], g1[:], Y[:, q, :], ALU.mult, ["g1", "Y"], ["g1"])
        actv(g1[:], g1[:], AF.Sigmoid, ["g1"], ["g1"], scale=1.5957691216)
        tt("dve", Y[:, q, :], Y[:, q, :], g1[:], ALU.mult, ["Y", "g1"], ["Y"])
        for f0 in range(0, 16, 4):
            hk = "psb" if (f0 // 4) % 2 == 0 else "psb"
            pv = psb[:, ((f0 // 4) % 2) * 512:((f0 // 4) % 2) * 512 + 512]
            for f in range(f0, f0 + 4):
                tr(pv[:, (f - f0) * 128:(f - f0 + 1) * 128], Y[:, q, f * 128:(f + 1) * 128], ident[:], ["Y", "ident"], [hk])
            actv(BsT[:, f0:f0 + 4, q * 128:(q + 1) * 128], pv.rearrange("p (f t) -> p f t", t=128), AF.Copy, [hk], ["BsT"])
    bsscr3 = bsscr.rearrange("c p t -> p c t")
    dma("sp", bsscr3, BsT[:], ["BsT"], ["bsscr"])
    dump("BsT", BsT[:], [128, 16, NOWN], BF16, ["BsT"])
    barrier()
    A.off = persist

    AT = A.alloc("AT", [128, 16, NOWN], BF16)
    w1k = A.alloc("w1k", [128, 32, 128], BF16)
    w1v = w1k
    w2k = A.alloc("w2k", [128, 128], BF16)
    w2v = A.alloc("w2v", [128, 128], BF16)
    dma("pool", w2k[:], d_wck2, (), ["w2k"])
    dma("pool", w2v[:], d_wcv2, (), ["w2v"])
    pekf = load("pekf", [128, 32], d_pek)
    pevf = load("pevf", [128, 32], d_pev)
    pekb = A.alloc("pekb", [128, 32], BF16)
    pevb = A.alloc("pevb", [128, 32], BF16)
    cp("dve", pekb[:], pekf[:], ["pekf"], ["pekb"])
    cp("dve", pevb[:], pevf[:], ["pevf"], ["pevb"])
    ovf = load("ovf", [128, 2, 64], d_ov)
    ovb = A.alloc("ovb", [128, 2, 64], BF16)
    cp("dve", ovb[:], ovf[:], ["ovf"], ["ovb"])
    Esel = A.alloc("Esel", [64, 32, 128], BF16)
    for kc4 in range(0, 32, 8):
        dma("pool", Esel[:, kc4:kc4 + 8, :], d_E[:, kc4:kc4 + 8, :], (), ["Esel"])
    tri4 = A.alloc("tri4", [128, 128], BF16)
    tris4 = A.alloc("tris4", [128, 128], BF16)
    mc4 = A.alloc("mc4", [128, 8, 128], BF16)
    dma("pool", tri4[:], d_tri, (), ["tri4"])
    dma("pool", tris4[:], d_tris, (), ["tris4"])
    dma("pool", mc4[:], d_mc4, (), ["mc4"])
    cmpb = load("cmpb", [128, 2], d_cmpb)
    winb = load("winb", [128, 12], d_winb)
    scmul = load("scmul", [128, 8, 64], d_scmul)
    scadd = load("scadd", [128, 8, 64], d_scadd)
    scval = load("scval", [128, 8, 64], d_scval)
    KcT = A.alloc("KcT", [128, SEQ], BF16)
    VcT = A.alloc("VcT", [128, SEQ], BF16)
    KsT = A.alloc("KsT", [128, SEQ], BF16)
    KwT = A.alloc("KwT", [128, 1536], BF16)
    Vs = A.alloc("Vs", [128, 32, 130], BF16)
    Vw = A.alloc("Vw", [128, 12, 130], BF16)
    QT = A.alloc("QT", [128, 8, 4, 128], BF16)
    Gt = A.alloc("Gt", [128, 8, 12], F32)
    kcT = A.alloc("kcT", [128, 256], BF16)
    vcm = A.alloc("vcm", [128, 2, 130], BF16)
    hidT = A.alloc("hidT", [128, 256], BF16)
    hb = A.alloc("hb", [128, 256], F32)
    hg = A.alloc("hg", [128, 256], F32)
    cbias = A.alloc("cbias", [128, 2], F32)
    Pc = A.alloc("Pc", [128, 2, 512], BF16)
    Pb = [A.alloc("Pb", [128, 512], BF16) for _ in range(2)]
    selT4 = A.alloc("selT4", [64, 4, 128], BF16)
    selneg = A.alloc("selneg", [128, 64], BF16)
    sc = A.alloc("sc", [128, 64], F32)
    sc2 = A.alloc("sc2", [128, 64], F32)
    m8 = A.alloc("m8", [128, 8], F32)
    thr = A.alloc("thr", [128, 1], F32)
    rden = A.alloc("rden", [128, 4], F32)
    coef = A.alloc("coef", [128, 4], F32)
    yat = A.alloc("yat", [128, 4, 128], F32)
    yab = A.alloc("yab", [128, 4, 128], BF16)
    ms("dve", Vs[:, :, 128:130], 1.0, ["Vs"])
    ms("dve", Vw[:, :, 128:130], 1.0, ["Vw"])
    ms("dve", vcm[:, :, 128:130], 1.0, ["vcm"])
    ms("dve", hidT[:], 0.0, ["hidT"])
    ms("dve", kcT[:], 0.0, ["kcT"])
    def gelu_evac(psrc, pkey, bias_ap, bkey):
        actv(hb[:, 0:255], psrc, AF.Identity, [pkey, bkey], ["hb"], bias=bias_ap)
        tt("dve", hg[:, 0:255], hb[:, 0:255], hb[:, 0:255], ALU.mult, ["hb"], ["hg"])
        ts("dve", hg[:, 0:255], hg[:, 0:255], 0.044715, 1.0, ALU.mult, ALU.add, ["hg"], ["hg"])
        tt("dve", hg[:, 0:255], hg[:, 0:255], hb[:, 0:255], ALU.mult, ["hg", "hb"], ["hg"])
        actv(hg[:, 0:255], hg[:, 0:255], AF.Sigmoid, ["hg"], ["hg"], scale=1.5957691216)
        tt("dve", hidT[:, 0:255], hb[:, 0:255], hg[:, 0:255], ALU.mult, ["hb", "hg"], ["hidT"])

    for g in range(4):
        gk = [f"kscr{g}"]
        dma("sp", KcT[:], kscr[g, 0], gk, ["KcT"])
        dma("sp", VcT[:], kscr[g, 1], gk, ["VcT"])
        dma("sp", KsT[:], kscr[g, 2], gk, ["KsT"])
        dma("sp", KwT[:], kwscr[g], gk, ["KwT"])
        dma("sp", Vs[:, :, 0:128], vsscr[g], gk, ["Vs"])
        dma("sp", Vw[:, :, 0:128], vwscr[g], gk, ["Vw"])
        dma("sp", QT[:], qscr[g], gk, ["QT"])
        dma("sp", Gt[:], gscr[g], gk, ["Gt"])
        for (w1, w1n, peb, pebn, src, srcn, w2, w2n, isk) in [(w1k, "w1k", pekb, "pekb", KcT, "KcT", w2k, "w2k", True),
                                                          (w1v, "w1v", pevb, "pevb", VcT, "VcT", w2v, "w2v", False)]:
            dma("pool", w1[:], (d_wck1 if isk else d_wcv1).rearrange("l d e -> d l e"), (), ["w1k"])
            w1n = "w1k"
            for l in range(32):
                mm(ps[5][:, 0:1], w1[:, l, :], peb[:, l:l + 1], l == 0, l == 31, [w1n, pebn], ["ps5"])
            cp("dve", cbias[:, 0:1], ps[5][:, 0:1], ["ps5"], ["cbias"])
            for l in range(32):
                mm(ps[6][:, 0:255], w1[:, l, :], (src[:].rearrange("p (n s) -> p n s", s=16)[:, 0:255, l] if l < 16 else src[:].rearrange("p (n s) -> p n s", s=16)[:, 1:256, l - 16]), l == 0, l == 31, [w1n, srcn], ["ps6"])
            gelu_evac(ps[6][:, 0:255], "ps6", cbias[:, 0:1], "cbias")
            if isk:
                mm(ps[5][:, 0:256], w2[:, :], hidT[:, :], True, True, [w2n, "hidT"], ["ps5"])
                actv(kcT[:, 0:255], ps[5][:, 0:255], AF.Copy, ["ps5"], ["kcT"])
            else:
                for c in range(2):
                    mm(ps[5][:, c * 128:(c + 1) * 128], hidT[:, c * 128:(c + 1) * 128], w2[:, :], True, True, [w2n, "hidT"], ["ps5"])
                actv(vcm[:, :, 0:128], ps[5][:, 0:256].rearrange("p (c f) -> p c f", f=128), AF.Copy, ["ps5"], ["vcm"])
        for qi in range(8):
            Qt = QT[:, qi, :, :].rearrange("p h t -> p (h t)")
            OB = [2, 3, 5, 6]

            def Oh(h):
                return ps[OB[h]][:, 0:130]

            def Ok(h):
                return f"ps{OB[h]}"

            def finish_branch(br, first):
                for h in range(4):
                    ts("dve", rden[:, h:h + 1], Oh(h)[:, 128:129], 1e-30, None, ALU.max, None, [Ok(h)], ["rden"])
                S.add("dve", (lambda o: (lambda e: e.reciprocal(out=o, in_=o)))(rden[:]), ["rden"], ["rden"])
                tt("dve", coef[:], rden[:], Gt[:, qi, :].rearrange("p (h b) -> p h b", b=3)[:, :, br], ALU.mult, ["rden", "Gt"], ["coef"])
                for h in range(4):
                    if first:
                        ts("dve", yat[:, h, :], Oh(h)[:, 0:128], coef[:, h:h + 1], None, ALU.mult, None, [Ok(h), "coef"], ["yat"])
                    else:
                        stt("dve", yat[:, h, :], Oh(h)[:, 0:128], coef[:, h:h + 1], yat[:, h, :], ALU.mult, ALU.add, [Ok(h), "coef", "yat"], ["yat"])

            for c in range(2):
                mm(ps[c][:, :], kcT[:, c * 128:(c + 1) * 128], Qt, True, True, ["kcT", "QT"], [f"ps{c}"])
                actv(Pc[:, c, :], ps[c][:, :], AF.Exp, [f"ps{c}", "cmpb"], ["Pc"], bias=cmpb[:, c:c + 1])
            tt("dve", Pc[:, 1, :].rearrange("p (h t) -> p h t", t=128), Pc[:, 1, :].rearrange("p (h t) -> p h t", t=128), mc4[:, qi, :].unsqueeze(1).to_broadcast([128, 4, 128]), ALU.mult, ["Pc", "mc4"], ["Pc"])
            for h in range(4):
                for c in range(2):
                    mm(Oh(h)[:, 0:129], Pc[:, c, h * 128:(h + 1) * 128], vcm[:, c, 0:129], c == 0, c == 1, ["Pc", "vcm"], [Ok(h)])
                    mm(ps[4][:, h * 64:(h + 1) * 64], Pc[:, c, h * 128:(h + 1) * 128], ovb[:, c, :], c == 0, c == 1, ["Pc", "ovb"], ["ps4"])
            finish_branch(0, True)
            for h in range(4):
                if h == 0:
                    ts("dve", sc[:], ps[4][:, 0:64], rden[:, 0:1], None, ALU.mult, None, ["ps4", "rden"], ["sc"])
                else:
                    stt("dve", sc[:], ps[4][:, h * 64:(h + 1) * 64], rden[:, h:h + 1], sc[:], ALU.mult, ALU.add, ["ps4", "rden", "sc"], ["sc"])
            tt("dve", sc[:], sc[:], scmul[:, qi, :], ALU.mult, ["sc", "scmul"], ["sc"])
            tt("dve", sc[:], sc[:], scadd[:, qi, :], ALU.add, ["sc", "scadd"], ["sc"])
            S.add("dve", lambda e: e.max(out=m8[:], in_=sc[:]), ["sc"], ["m8"])
            S.add("dve", lambda e: e.match_replace(out=sc2[:], in_to_replace=m8[:], in_values=sc[:], imm_value=-3.0e6), ["sc", "m8"], ["sc2"])
            S.add("dve", lambda e: e.max(out=m8[:], in_=sc2[:]), ["sc2"], ["m8"])
            S.add("dve", lambda e: e.tensor_reduce(out=thr[:], in_=m8[:], axis=AX.X, op=ALU.min), ["m8"], ["thr"])
            ts("dve", sc2[:], sc[:], thr[:, 0:1], None, ALU.is_ge, None, ["sc", "thr"], ["sc2"])
            tt("dve", sc2[:], sc2[:], scval[:, qi, :], ALU.mult, ["sc2", "scval"], ["sc2"])
            ts("dve", selneg[:], sc2[:], -1.0, -NEGM, ALU.add, ALU.mult, ["sc2"], ["selneg"])
            tr(psb[0:64, 0:128], selneg[:, :], ident[:], ["selneg", "ident"], ["psb"])
            for h in range(4):
                cp("dve", selT4[:, h, :], psb[0:64, 0:128], ["psb"], ["selT4"])
            selflat = selT4[:].rearrange("p h t -> p (h t)")
            nk = 24 + qi + 1

            def pv_sel(kc):
                P_ = Pb[kc % 2]
                Pk = f"Pb{kc % 2}"
                for h in range(4):
                    mm(Oh(h)[:, 0:129], P_[:, h * 128:(h + 1) * 128], Vs[:, kc, 0:129], kc == 0, kc == nk - 1, [Pk, "Vs"], [Ok(h)])

            pend = None
            for kc in range(nk):
                pS = ps[kc % 2]
                pk = f"ps{kc % 2}"
                mm(pS[:, :], KsT[:, kc * 128:(kc + 1) * 128], Qt, True, False, ["KsT", "QT"], [pk])
                mm(pS[:, :], Esel[:, kc, :], selflat, False, True, ["Esel", "selT4"], [pk])
                P_ = Pb[kc % 2]
                Pk = f"Pb{kc % 2}"
                actv(P_[:], pS[:, :], AF.Exp, [pk], [Pk])
                if kc == nk - 1:
                    tt("dve", P_[:].rearrange("p (h t) -> p h t", t=128), P_[:].rearrange("p (h t) -> p h t", t=128), tri4[:].unsqueeze(1).to_broadcast([128, 4, 128]), ALU.mult, [Pk, "tri4"], [Pk])
                if pend is not None:
                    pv_sel(pend)
                pend = kc
            pv_sel(pend)
            finish_branch(1, False)
            def pv_win(wi):
                P_ = Pb[wi % 2]
                Pk = f"Pb{wi % 2}"
                kl = qi + wi
                for h in range(4):
                    mm(Oh(h)[:, 0:129], P_[:, h * 128:(h + 1) * 128], Vw[:, kl, 0:129], wi == 0, wi == 4, [Pk, "Vw"], [Ok(h)])

            pend = None
            for wi in range(5):
                kc = 20 + qi + wi
                kl = kc - 20
                pS = ps[wi % 2]
                pk = f"ps{wi % 2}"
                mm(pS[:, :], KwT[:, kl * 128:(kl + 1) * 128], Qt, True, True, ["KwT", "QT"], [pk])
                P_ = Pb[wi % 2]
                Pk = f"Pb{wi % 2}"
                actv(P_[:], pS[:, :], AF.Exp, [pk, "winb"], [Pk], bias=winb[:, kl:kl + 1])
                if wi == 0:
                    tt("dve", P_[:].rearrange("p (h t) -> p h t", t=128), P_[:].rearrange("p (h t) -> p h t", t=128), tris4[:].unsqueeze(1).to_broadcast([128, 4, 128]), ALU.mult, [Pk, "tris4"], [Pk])
                if wi == 4:
                    tt("dve", P_[:].rearrange("p (h t) -> p h t", t=128), P_[:].rearrange("p (h t) -> p h t", t=128), tri4[:].unsqueeze(1).to_broadcast([128, 4, 128]), ALU.mult, [Pk, "tri4"], [Pk])
                if pend is not None:
                    pv_win(pend)
                pend = wi
            pv_win(pend)
            finish_branch(2, False)
            cp("dve", yab[:], yat[:], ["yat"], ["yab"])
            for h in range(4):
                tr(psb[:, 512 + h * 128:512 + (h + 1) * 128], yab[:, h, :], ident[:], ["yab", "ident"], ["psb"])
            actv(AT[:, 4 * g:4 * g + 4, qi * 128:(qi + 1) * 128], psb[:, 512:1024].rearrange("p (h t) -> p h t", t=128), AF.Copy, ["psb"], ["AT"])
    if DEBUG:
        dma("sp", dbg["yaT"], AT[:], ["AT"], ["dbg_yaT"])
        dbg_written.append("dbg_yaT")
    if stop == "s5":
        return finish()
    barrier()
    at_end = persist + 16 * NOWN * 2
    A.off = at_end

    mT = A.alloc("mT", [128, 16, NOWN], BF16)
    mt_end = A.off
    BsT = A.alloc("BsT2", [128, 16, NOWN], BF16)
    dma("sp", BsT[:], bsscr3, ["bsscr"], ["BsT"])
    uTo = A.alloc("uTo", [128, 16, NOWN], BF16)
    dma("sp", uTo[:], uscr3[:, :, SEQ - NOWN:SEQ], ["uscr"], ["uTo"])
    wblk = [A.alloc("wblk", [128, 16, 512], BF16) for _ in range(3)]
    e1 = A.alloc("e1", [128, 512], F32)
    e2 = A.alloc("e2", [128, 512], F32)
    e3 = A.alloc("e3", [128, 512], F32)
    wi_ = 0

    def wload(src3, c0, ncols=512):
        nonlocal wi_
        w_ = wblk[wi_ % len(wblk)]
        wk = f"wblk{wi_ % len(wblk)}"
        wi_ += 1
        dma("pool", w_[:, :, 0:ncols], src3[:, :, c0:c0 + ncols], (), [wk])
        return w_, wk

    d_wglu3 = d_wglu.rearrange("(k p) n -> p k n", p=128)
    d_wout3 = d_wout.rearrange("(k p) n -> p k n", p=128)
    d_wup3 = d_wup.rearrange("(k p) n -> p k n", p=128)
    for nb in range(4):
        wg, wgk = wload(d_wglu3, nb * 512)
        wa_, wak = wload(d_win3, 7216 + nb * 512)
        wb_, wbk = wload(d_win3, 9264 + nb * 512)
        for cc in range(4):
            n = nb * 4 + cc
            for th in range(2):
                tsl = slice(th * 512, (th + 1) * 512)
                for k in range(16):
                    mm(ps[0][:, :], wg[:, k, cc * 128:(cc + 1) * 128], BsT[:, k, tsl], k == 0, k == 15, [wgk, "BsT"], ["ps0"])
                actv(e1[:], ps[0][:, :], AF.Sigmoid, ["ps0", "bglu"], ["e1"], bias=bglu[:, n:n + 1])
                for k in range(16):
                    mm(ps[1][:, :], wa_[:, k, cc * 128:(cc + 1) * 128], uTo[:, k, tsl], k == 0, k == 15, [wak, "uTo"], ["ps1"])
                actv(e2[:], ps[1][:, :], AF.Sigmoid, ["ps1"], ["e2"])
                for k in range(16):
                    mm(ps[2][:, :], wb_[:, k, cc * 128:(cc + 1) * 128], uTo[:, k, tsl], k == 0, k == 15, [wbk, "uTo"], ["ps2"])
                actv(e3[:], ps[2][:, :], AF.Sigmoid, ["ps2"], ["e3"])
                tt("dve", e1[:], e1[:], BsT[:, n, tsl], ALU.mult, ["e1", "BsT"], ["e1"])
                tt("dve", e1[:], e1[:], e3[:], ALU.mult, ["e1", "e3"], ["e1"])
                tt("dve", e2[:], e2[:], AT[:, n, tsl], ALU.mult, ["e2", "AT"], ["e2"])
                tt("dve", mT[:, n, tsl], e1[:], e2[:], ALU.add, ["e1", "e2"], ["mT"])
    barrier()
    A.off = mt_end
    mT2 = mT
    h1T = A.alloc("h1T", [128, 16, NOWN], F32)
    sq = A.alloc("sq", [128, 512], F32)
    rsb = A.alloc("rsb", [128, NOWN], F32)
    rs_end = A.off
    xo = A.alloc("xo", [128, D], F32)
    wblk = [A.alloc("wblk", [128, 16, 512], BF16) for _ in range(2)]
    for q in range(8):
        dma("sp", xo[:], d_xw[SEQ - NOWN + q * 128:SEQ - NOWN + (q + 1) * 128, :], (), ["xo"])
        for f0 in range(0, 16, 4):
            for f in range(f0, f0 + 4):
                tr(ps[4][:, (f - f0) * 128:(f - f0 + 1) * 128], xo[:, f * 128:(f + 1) * 128], identf[:], ["xo", "identf"], ["ps4"])
            actv(h1T[:, f0:f0 + 4, q * 128:(q + 1) * 128], ps[4][:, :].rearrange("p (f t) -> p f t", t=128), AF.Copy, ["ps4"], ["h1T"])
    for nb in range(4):
        wo, wok = wload(d_wout3, nb * 512)
        for cc in range(4):
            n = nb * 4 + cc
            for th in range(2):
                tsl = slice(th * 512, (th + 1) * 512)
                for k in range(16):
                    mm(ps[th][:, :], wo[:, k, cc * 128:(cc + 1) * 128], mT2[:, k, tsl], k == 0, k == 15, [wok, "mT"], [f"ps{th}"])
                stt("dve", h1T[:, n, tsl], ps[th][:, :], ada[:, 32 + n:33 + n], h1T[:, n, tsl], ALU.mult, ALU.add, [f"ps{th}", "ada", "h1T"], ["h1T"])
    def feat_rstd(srcT, skey):
        for th in range(2):
            tsl = slice(th * 512, (th + 1) * 512)
            for n in range(16):
                actv(sq[:], srcT[:, n, tsl], AF.Square, [skey], ["sq"])
                mm(ps[3][:, :], onesf[:, :], sq[:], n == 0, n == 15, ["onesf", "sq"], ["ps3"])
            ts("dve", rsb[:, tsl], ps[3][:, :], 1.0 / D, EPS, ALU.mult, ALU.add, ["ps3"], ["rsb"])
            actv(rsb[:, tsl], rsb[:, tsl], AF.Sqrt, ["rsb"], ["rsb"])
            S.add("dve", (lambda o: (lambda e: e.reciprocal(out=o, in_=o)))(rsb[:, tsl]), ["rsb"], ["rsb"])

    dump("h1T", h1T[:], [128, 16, NOWN], F32, ["h1T"])
    dump("mT", mT[:], [128, 16, NOWN], BF16, ["mT"])
    feat_rstd(h1T, "h1T")
    u2T = AT
    barrier()
    for n in range(16):
        for th in range(2):
            tsl = slice(th * 512, (th + 1) * 512)
            tt("dve", sq[:], h1T[:, n, tsl], rsb[:, tsl], ALU.mult, ["h1T", "rsb"], ["sq"])
            actv(u2T[:, n, tsl], sq[:], AF.Identity, ["sq", "m2s", "ada"], ["u2T"], bias=ada[:, 48 + n:49 + n], scale=m2s[:, n:n + 1])
    dump("u2T", u2T[:], [128, 16, NOWN], BF16, ["u2T"])
    h1scr3 = h1scr.rearrange("c p t -> p c t")
    dma("sp", h1scr3, h1T[:], ["h1T"], ["h1scr"])
    barrier()
    acc = h1T
    A.off = at_end
    wdb = [A.alloc("wdb", [128, 4, D], BF16) for _ in range(2)]
    assert A.off <= mt_end
    A.off = rs_end
    hid = A.alloc("hid", [128, 4, NOWN], BF16)
    rl = [A.alloc("rl", [128, 512], F32) for _ in range(2)]
    wblk = [A.alloc("wblk", [128, 16, 512], BF16) for _ in range(2)]
    d_wdn3 = d_wdn.rearrange("(k p) n -> p k n", p=128)
    for fb in range(16):
        wu, wuk = wload(d_wup3, fb * 512)
        wd = wdb[fb % 2]
        wdk = f"wdb{fb % 2}"
        dma("pool", wd[:], d_wdn3[:, fb * 4:fb * 4 + 4, :], (), [wdk])
        for cc in range(4):
            for th in range(2):
                tsl = slice(th * 512, (th + 1) * 512)
                p_ = ps[(cc * 2 + th) % 2]
                pk = f"ps{(cc * 2 + th) % 2}"
                for k in range(16):
                    mm(p_[:, :], wu[:, k, cc * 128:(cc + 1) * 128], u2T[:, k, tsl], k == 0, k == 15, [wuk, "u2T"], [pk])
                rl_ = rl[(cc * 2 + th) % 2]
                rk = f"rl{(cc * 2 + th) % 2}"
                actv(rl_[:], p_[:, :], AF.Relu, [pk], [rk])
                tt("dve", hid[:, cc, tsl], rl_[:], rl_[:], ALU.mult, [rk], ["hid"])
        for n in range(16):
            for th in range(2):
                tsl = slice(th * 512, (th + 1) * 512)
                p_ = ps[2 + (n * 2 + th) % 2]
                pk = f"ps{2 + (n * 2 + th) % 2}"
                for k in range(4):
                    mm(p_[:, :], wd[:, k, n * 128:(n + 1) * 128], hid[:, k, tsl], k == 0, k == 3, [wdk, "hid"], [pk])
                if fb == 0:
                    cp("dve", acc[:, n, tsl], p_[:, :], [pk], ["acc"])
                else:
                    tt("dve", acc[:, n, tsl], acc[:, n, tsl], p_[:, :], ALU.add, [pk, "acc"], ["acc"])
    barrier()
    A.off = rs_end
    hch = [A.alloc("hch", [128, NOWN], F32) for _ in range(2)]
    for n in range(16):
        hc = hch[n % 2]
        hk = f"hch{n % 2}"
        dma("sp", hc[:], h1scr3[:, n, :], ["h1scr"], [hk])
        stt("dve", acc[:, n, :], acc[:, n, :], ada[:, 80 + n:81 + n], hc[:], ALU.mult, ALU.add, ["acc", "ada", hk], ["acc"])
    dump("h2T", acc[:], [128, 16, NOWN], F32, ["acc"])
    feat_rstd(acc, "acc")
    ot = A.alloc("ot", [128, 16, 128], F32)
    orow = [A.alloc("orow", [128, D], F32) for _ in range(2)]
    for q in range(8):
        qs = slice(q * 128, (q + 1) * 128)
        for n in range(16):
            stt("dve", ot[:, n, :], acc[:, n, qs], gfin[:, n:n + 1], rsb[:, qs], ALU.mult, ALU.mult, ["acc", "gfin", "rsb"], ["ot"])
        orw = orow[q % 2]
        ok_ = f"orow{q % 2}"
        for f0 in range(0, 16, 4):
            for f in range(f0, f0 + 4):
                tr(ps[4][:, (f - f0) * 128:(f - f0 + 1) * 128], ot[:, f, :], identf[:], ["ot", "identf"], ["ps4"])
            actv(orw[:, f0 * 128:(f0 + 4) * 128], ps[4][:, :], AF.Copy, ["ps4"], [ok_])
        dma("sp", d_out[qs, :], orw[:], [ok_], [f"out{q}"])
    return finish()


def _prep(inputs):
    f32 = np.float32
    x = np.asarray(inputs["x"], f32)
    c = np.asarray(inputs["c"], f32)

    def col(v, n):
        return np.ascontiguousarray(np.asarray(v, f32).reshape(n, 128).T)

    a_re = np.asarray(inputs["a_re"], f32)[0]
    a_im = np.asarray(inputs["a_im"], f32)[0]
    log_dt = np.asarray(inputs["log_dt"], f32)[0]
    b_re = np.asarray(inputs["b_re"], f32)[0]
    b_im = np.asarray(inputs["b_im"], f32)[0]
    c_re = np.asarray(inputs["c_re"], f32)[0]
    c_im = np.asarray(inputs["c_im"], f32)[0]
    d_skip = np.asarray(inputs["d_skip"], f32)[0]
    def pairlay(v):
        sh = v.shape
        return np.ascontiguousarray(v.reshape((64, 2) + sh[1:]).transpose((1, 2, 0) + tuple(range(3, len(sh) + 1))).reshape((128, 64) + sh[2:]))

    HCR = np.zeros((2, 64, 64, 32), f32)
    HCI = np.zeros((2, 64, 64, 32), f32)
    Dm = np.zeros((32, 64, 32), f32)
    cr = c_re.reshape(64, 2, 16, 64)
    ci_ = c_im.reshape(64, 2, 16, 64)
    for e_ in range(2):
        HCR[e_, :, :, 16 * e_:16 * e_ + 16] = cr[:, e_].transpose(2, 0, 1)
        HCI[e_, :, :, 16 * e_:16 * e_ + 16] = ci_[:, e_].transpose(2, 0, 1)
        for cc in range(16):
            Dm[16 * e_ + cc, :, 16 * e_ + cc] = d_skip.reshape(64, 2, 16)[:, e_, cc]
    shared = {
        "w_ada": np.ascontiguousarray(inputs["w_ada"][0], dtype=f32),
        "bada": col(inputs["b_ada"][0], 96),
        "gmix": col(inputs["g_mix"][0], 16),
        "gmlp": col(inputs["g_mlp"][0], 16),
        "gfin": col(inputs["g_final"], 16),
        "bglu": col(inputs["b_glu"][0], 16),
        "w_in": np.ascontiguousarray(inputs["w_in"][0], dtype=f32),
        "w_ck1": np.ascontiguousarray(inputs["w_ck1"][0], dtype=f32),
        "w_ck2": np.ascontiguousarray(inputs["w_ck2"][0], dtype=f32),
        "pek": np.ascontiguousarray(np.asarray(inputs["pe_ck"][0], f32).T),
        "w_cv1": np.ascontiguousarray(inputs["w_cv1"][0], dtype=f32),
        "w_cv2": np.ascontiguousarray(inputs["w_cv2"][0], dtype=f32),
        "pev": np.ascontiguousarray(np.asarray(inputs["pe_cv"][0], f32).T),
        "are2": pairlay(a_re),
        "aim2": pairlay(a_im),
        "ldt2": pairlay(np.ascontiguousarray(np.broadcast_to(log_dt[:, None], (128, 64)))),
        "X1": pairlay(b_re),
        "X2": pairlay(b_im),
        "CRI": np.ascontiguousarray(HCR.reshape(128, 64, 32)),
        "CIR": np.ascontiguousarray(HCI.reshape(128, 64, 32)),
        "Dm": Dm,
        "w_glu": np.ascontiguousarray(inputs["w_glu"][0], dtype=f32),
        "w_out": np.ascontiguousarray(inputs["w_out"][0], dtype=f32),
        "w_up": np.ascontiguousarray(inputs["w_up"][0], dtype=f32),
        "w_down": np.ascontiguousarray(inputs["w_down"][0], dtype=f32),
    }
    n = np.arange(256)
    jb = np.arange(64)
    ovm = ((16 * n[:, None] < 64 * jb[None, :] + 64) & (16 * n[:, None] + 32 > 64 * jb[None, :]) & (n[:, None] <= 254)).astype(f32)
    shared["ov"] = np.ascontiguousarray(ovm.reshape(2, 128, 64).transpose(1, 0, 2))
    E = np.zeros((64, 32, 128), f32)
    for kc in range(32):
        for k in range(128):
            E[2 * kc + k // 64, kc, k] = 1.0
    shared["Esel"] = E
    kk = np.arange(128)[:, None]
    ql = np.arange(128)[None, :]
    shared["tri4"] = np.ascontiguousarray((kk <= ql).astype(f32))
    shared["tris4"] = np.ascontiguousarray((kk > ql).astype(f32))
    mc = np.zeros((128, 8, 128), f32)
    for qi in range(8):
        mc[:, qi, :] = (16 * kk - ql <= 993 + 128 * qi).astype(f32)
    shared["mc4"] = mc
    maps = []
    for core in range(8):
        b, j = core // 4, core % 4
        P = 1024 * (3 - j)
        xw = np.zeros((SEQ, D), f32)
        xw[P:] = x[b, :SEQ - P]
        m = dict(shared)
        m["xw"] = xw
        m["ccol"] = col(c[b], 16)
        tau = np.arange(SEQ)
        m["mtok"] = np.ascontiguousarray(np.broadcast_to((tau >= P).astype(f32)[None, :], (128, SEQ)))
        ncmp = np.arange(256).reshape(2, 128).T
        m["cmpb"] = np.where((ncmp <= 254) & (16 * ncmp >= P), 0.0, NEGM).astype(f32)
        kw = (128 * (20 + np.arange(12))[None, :] + np.arange(128)[:, None])
        m["winb"] = np.where(kw >= P, 0.0, NEGM).astype(f32)
        f0 = P // 64
        scmul = np.zeros((128, 8, 64), f32)
        scadd = np.zeros((128, 8, 64), f32)
        scval = np.zeros((128, 8, 64), f32)
        for qi in range(8):
            t = 3072 + 128 * qi + np.arange(128)
            cur = (t // 64)[:, None]
            jj = jb[None, :]
            valid = (jj >= f0) & (jj <= cur)
            forced = valid & ((jj == f0) | (jj == cur) | (jj == cur - 1))
            scmul[:, qi, :] = (valid & ~forced)
            scadd[:, qi, :] = np.where(forced, 1.0e6 + jj, np.where(valid, 0.0, -1.0e6 - jj))
            scval[:, qi, :] = valid
        m["scmul"], m["scadd"], m["scval"] = scmul, scadd, scval
        maps.append(m)
    return maps


_CACHE = {}


def kernel(**inputs):
    maps = _prep(inputs)
    if "nc" not in _CACHE:
        _CACHE["nc"] = build()
    nc, S = _CACHE["nc"]
    res = run_bass_kernel_spmd(nc, maps, core_ids=list(range(8)))
    out = np.zeros((2, SEQ, D), np.float32)
    for core in range(8):
        b, j = core // 4, core % 4
        out[b, 1024 * j:1024 * (j + 1)] = res.results[core]["out"]
    if DEBUG:
        _CACHE["dbg"] = res.results
    return out
```
